# Optimizing a Trainium2 kernel written in Bass

```python
import math
import jax, jax.numpy as jnp
from jax import lax
import numpy as np

D_MODEL = 1024
BATCH = 4
SEQ = 8192
DEPTH = 4

D_FF = 2816
N_DIFF_HEADS = 4
DIFF_HEAD_DIM = 64
DIFF_WIDTH = N_DIFF_HEADS * 2 * DIFF_HEAD_DIM
Q_BLOCK = 128
NUM_BUCKETS = 32
MAX_DISTANCE = 128
SSM_HEADS = 8
SSM_HEAD_DIM = 64
SSM_WIDTH = SSM_HEADS * SSM_HEAD_DIM
SSM_GROUPS = 2
SSM_STATE = 128
SSM_CONV = 4
SSM_CHUNK = 128
HEADS_PER_GROUP = SSM_HEADS // SSM_GROUPS
SSM_XBC = SSM_WIDTH + 2 * SSM_GROUPS * SSM_STATE
QK_WIDTH = N_DIFF_HEADS * 2 * DIFF_HEAD_DIM
HYB_SPLITS = (QK_WIDTH, 2 * QK_WIDTH, 2 * QK_WIDTH + DIFF_WIDTH,
              2 * QK_WIDTH + DIFF_WIDTH + SSM_WIDTH,
              2 * QK_WIDTH + DIFF_WIDTH + SSM_WIDTH + SSM_XBC)
HYB_IN = HYB_SPLITS[-1] + SSM_HEADS
MIX_WIDTH = DIFF_WIDTH + SSM_WIDTH
SC_WIDTH = D_MODEL
SC_CONV = 3

kernel_name = "hybrid_diffattn_ssd_shortconv_macaron"


def rms_norm(x, w, eps=1e-6):
    xf = x.astype(jnp.float32)
    y = xf * lax.rsqrt(jnp.mean(xf * xf, axis=-1, keepdims=True) + eps)
    return (y * w.astype(jnp.float32)).astype(x.dtype)


def swiglu(h, wg, wu, wd):
    return (jax.nn.silu(h @ wg) * (h @ wu)) @ wd


def causal_dwconv(x, w):
    K, C = w.shape
    return lax.conv_general_dilated(
        x, w[:, None, :].astype(x.dtype), window_strides=(1,), padding=[(K - 1, 0)],
        dimension_numbers=("NWC", "WIO", "NWC"), feature_group_count=C)


def t5_bucket(dist):
    max_exact = NUM_BUCKETS // 2
    is_small = dist < max_exact
    large = max_exact + (
        jnp.log(jnp.maximum(dist, 1).astype(jnp.float32) / max_exact)
        / math.log(MAX_DISTANCE / max_exact) * (NUM_BUCKETS - max_exact)).astype(jnp.int32)
    large = jnp.minimum(large, NUM_BUCKETS - 1)
    return jnp.where(is_small, dist, large)


def diff_attention(q1, q2, k1, k2, v, lam, rel_bias):
    Bsz, S, H, d = q1.shape
    nblk = S // Q_BLOCK
    scale = d ** -0.5
    kpos = jnp.arange(S, dtype=jnp.int32)
    to_blocks = lambda t: jnp.swapaxes(t.reshape(Bsz, nblk, Q_BLOCK, H, d), 0, 1)
    starts = jnp.arange(nblk, dtype=jnp.int32) * Q_BLOCK
    table = rel_bias.astype(jnp.float32)

    def block(args):
        qb1, qb2, start = args
        qpos = start + jnp.arange(Q_BLOCK, dtype=jnp.int32)
        dist = qpos[:, None] - kpos[None, :]
        causal = dist >= 0
        bias = jnp.transpose(table[t5_bucket(jnp.maximum(dist, 0))], (2, 0, 1))
        s1 = jnp.einsum("bqhd,bkhd->bhqk", qb1, k1) * scale + bias
        s2 = jnp.einsum("bqhd,bkhd->bhqk", qb2, k2) * scale + bias
        p = (jax.nn.softmax(jnp.where(causal, s1, -jnp.inf), axis=-1)
             - lam * jax.nn.softmax(jnp.where(causal, s2, -jnp.inf), axis=-1))
        return jnp.einsum("bhqk,bkhe->bqhe", p, v)

    out = lax.map(block, (to_blocks(q1), to_blocks(q2), starts))
    return jnp.swapaxes(out, 0, 1).reshape(Bsz, S, H, 2 * d)


def segsum(a):
    T = a.shape[-1]
    cs = jnp.cumsum(a, axis=-1)
    seg = cs[..., :, None] - cs[..., None, :]
    mask = jnp.tril(jnp.ones((T, T), dtype=bool))
    return jnp.where(mask, seg, -jnp.inf)


def ssd_chunked(X, A, Bm, Cm):
    b, S, h, p = X.shape
    n = Bm.shape[-1]
    c = S // SSM_CHUNK
    X = X.reshape(b, c, SSM_CHUNK, h, p)
    Bm = Bm.reshape(b, c, SSM_CHUNK, h, n)
    Cm = Cm.reshape(b, c, SSM_CHUNK, h, n)
    A = jnp.transpose(A.reshape(b, c, SSM_CHUNK, h), (0, 3, 1, 2))
    A_cs = jnp.cumsum(A, axis=-1)
    L = jnp.exp(segsum(A))
    CB = jnp.einsum("bclhn,bcshn->bhcls", Cm, Bm)
    y_diag = jnp.einsum("bhcls,bcshp->bclhp", CB * L, X)
    decay_states = jnp.exp(A_cs[..., -1:] - A_cs)
    states = jnp.einsum("bclhn,bhcl,bclhp->bchpn", Bm, decay_states, X)
    states = jnp.concatenate([jnp.zeros_like(states[:, :1]), states], axis=1)
    decay_chunk = jnp.exp(segsum(jnp.pad(A_cs[..., -1], ((0, 0), (0, 0), (1, 0)))))
    states = jnp.einsum("bhzc,bchpn->bzhpn", decay_chunk, states)[:, :-1]
    y_off = jnp.einsum("bclhn,bchpn,bhcl->bclhp", Cm, states, jnp.exp(A_cs))
    return (y_diag + y_off).reshape(b, S, h, p)


def gated_group_rms_norm(y, z, w, eps=1e-5):
    g = y.astype(jnp.float32) * jax.nn.silu(z.astype(jnp.float32))
    shp = g.shape
    g = g.reshape(*shp[:-1], SSM_GROUPS, shp[-1] // SSM_GROUPS)
    g = g * lax.rsqrt(jnp.mean(g * g, axis=-1, keepdims=True) + eps)
    return g.reshape(shp) * w.astype(jnp.float32)


def hybrid_mixer(h, w_in, w_out, lq1, lk1, lq2, lk2, subln_w, conv_w, conv_b,
                 dt_bias, a_log, d_skip, norm_w, rel_bias, lam_init):
    Bsz, S, _ = h.shape
    q, k, v, z, xbc, dt = jnp.split(h @ w_in, HYB_SPLITS, axis=-1)
    q = q.reshape(Bsz, S, N_DIFF_HEADS, 2, DIFF_HEAD_DIM).astype(jnp.float32)
    k = k.reshape(Bsz, S, N_DIFF_HEADS, 2, DIFF_HEAD_DIM).astype(jnp.float32)
    v = v.reshape(Bsz, S, N_DIFF_HEADS, 2 * DIFF_HEAD_DIM).astype(jnp.float32)
    f32 = lambda t: t.astype(jnp.float32)
    lam = (jnp.exp(jnp.sum(f32(lq1) * f32(lk1))) - jnp.exp(jnp.sum(f32(lq2) * f32(lk2))) + lam_init)
    o = diff_attention(q[..., 0, :], q[..., 1, :], k[..., 0, :], k[..., 1, :], v, lam, rel_bias)
    o = rms_norm(o, subln_w, eps=1e-5) * (1.0 - lam_init)
    o = o.reshape(Bsz, S, DIFF_WIDTH).astype(h.dtype)
    xbc = jax.nn.silu(causal_dwconv(xbc, conv_w) + conv_b)
    xs, bm, cm = jnp.split(xbc, (SSM_WIDTH, SSM_WIDTH + SSM_GROUPS * SSM_STATE), axis=-1)
    xs = xs.reshape(Bsz, S, SSM_HEADS, SSM_HEAD_DIM).astype(jnp.float32)
    bm = jnp.repeat(bm.reshape(Bsz, S, SSM_GROUPS, SSM_STATE).astype(jnp.float32), HEADS_PER_GROUP, axis=2)
    cm = jnp.repeat(cm.reshape(Bsz, S, SSM_GROUPS, SSM_STATE).astype(jnp.float32), HEADS_PER_GROUP, axis=2)
    dt = jax.nn.softplus(dt.astype(jnp.float32) + f32(dt_bias))
    a = -jnp.exp(f32(a_log))
    y = ssd_chunked(xs * dt[..., None], a * dt, bm, cm) + f32(d_skip)[:, None] * xs
    y = gated_group_rms_norm(y.reshape(Bsz, S, SSM_WIDTH), z, norm_w).astype(h.dtype)
    return jnp.concatenate([o, y], axis=-1) @ w_out


def short_conv_mixer(h, w_in, conv_w, w_out):
    bg, cg, u = jnp.split(h @ w_in, 3, axis=-1)
    return (bg * causal_dwconv(cg * u, conv_w)) @ w_out


def setup_inputs(seed: int = 0) -> dict:
    key = jax.random.key(seed)
    ks = iter(jax.random.split(key, 48))
    nrm = lambda shape, scale: scale * jax.random.normal(next(ks), shape, jnp.float32)
    gain = lambda shape: 1.0 + nrm(shape, 0.02)
    NE = (DEPTH + 1) // 2
    NO = DEPTH // 2
    dt0 = jnp.exp(jax.random.uniform(next(ks), (NE, SSM_HEADS), jnp.float32,
                                     math.log(1e-3), math.log(1e-1)))
    return {
        "x": nrm((BATCH, SEQ, D_MODEL), 1.0),
        "rel_bias": nrm((NUM_BUCKETS, N_DIFF_HEADS), 0.5),
        "final_norm_w": gain((D_MODEL,)),
        "ffn1_norm": gain((DEPTH, D_MODEL)),
        "ffn1_wg": nrm((DEPTH, D_MODEL, D_FF), D_MODEL ** -0.5),
        "ffn1_wu": nrm((DEPTH, D_MODEL, D_FF), D_MODEL ** -0.5),
        "ffn1_wd": nrm((DEPTH, D_FF, D_MODEL), D_FF ** -0.5),
        "mix_norm": gain((DEPTH, D_MODEL)),
        "ffn2_norm": gain((DEPTH, D_MODEL)),
        "ffn2_wg": nrm((DEPTH, D_MODEL, D_FF), D_MODEL ** -0.5),
        "ffn2_wu": nrm((DEPTH, D_MODEL, D_FF), D_MODEL ** -0.5),
        "ffn2_wd": nrm((DEPTH, D_FF, D_MODEL), D_FF ** -0.5),
        "hyb_w_in": nrm((NE, D_MODEL, HYB_IN), D_MODEL ** -0.5),
        "hyb_w_out": nrm((NE, MIX_WIDTH, D_MODEL), MIX_WIDTH ** -0.5),
        "diff_lq1": nrm((NE, DIFF_HEAD_DIM), 0.1),
        "diff_lk1": nrm((NE, DIFF_HEAD_DIM), 0.1),
        "diff_lq2": nrm((NE, DIFF_HEAD_DIM), 0.1),
        "diff_lk2": nrm((NE, DIFF_HEAD_DIM), 0.1),
        "diff_subln_w": gain((NE, 2 * DIFF_HEAD_DIM)),
        "ssm_conv_w": nrm((NE, SSM_CONV, SSM_XBC), SSM_CONV ** -0.5),
        "ssm_conv_b": nrm((NE, SSM_XBC), 0.02),
        "ssm_dt_bias": dt0 + jnp.log(-jnp.expm1(-dt0)),
        "ssm_a_log": jnp.log(jax.random.uniform(next(ks), (NE, SSM_HEADS), jnp.float32, 1.0, 16.0)),
        "ssm_d": 1.0 + nrm((NE, SSM_HEADS), 0.1),
        "ssm_norm_w": gain((NE, SSM_WIDTH)),
        "sc_w_in": nrm((NO, D_MODEL, 3 * SC_WIDTH), D_MODEL ** -0.5),
        "sc_conv_w": nrm((NO, SC_CONV, SC_WIDTH), SC_CONV ** -0.5),
        "sc_w_out": nrm((NO, SC_WIDTH, D_MODEL), SC_WIDTH ** -0.5),
    }


def reference(x, rel_bias, final_norm_w, ffn1_norm, ffn1_wg, ffn1_wu, ffn1_wd, mix_norm,
              ffn2_norm, ffn2_wg, ffn2_wu, ffn2_wd, hyb_w_in, hyb_w_out, diff_lq1, diff_lk1,
              diff_lq2, diff_lk2, diff_subln_w, ssm_conv_w, ssm_conv_b, ssm_dt_bias, ssm_a_log,
              ssm_d, ssm_norm_w, sc_w_in, sc_conv_w, sc_w_out):
    for i in range(DEPTH):
        j = i // 2
        x = x + 0.5 * swiglu(rms_norm(x, ffn1_norm[i]), ffn1_wg[i], ffn1_wu[i], ffn1_wd[i])
        h = rms_norm(x, mix_norm[i])
        if i % 2 == 0:
            lam_init = 0.8 - 0.6 * math.exp(-0.3 * i)
            x = x + hybrid_mixer(h, hyb_w_in[j], hyb_w_out[j], diff_lq1[j], diff_lk1[j],
                                 diff_lq2[j], diff_lk2[j], diff_subln_w[j], ssm_conv_w[j],
                                 ssm_conv_b[j], ssm_dt_bias[j], ssm_a_log[j], ssm_d[j],
                                 ssm_norm_w[j], rel_bias, lam_init)
        else:
            x = x + short_conv_mixer(h, sc_w_in[j], sc_conv_w[j], sc_w_out[j])
        x = x + 0.5 * swiglu(rms_norm(x, ffn2_norm[i]), ffn2_wg[i], ffn2_wu[i], ffn2_wd[i])
    return rms_norm(x, final_norm_w)
```

```python
import math
from contextlib import ExitStack

import numpy as np
import concourse.bass as bass
import concourse.mybir as mybir
from concourse.bass_utils import run_bass_kernel_spmd

F32 = mybir.dt.float32
BF16 = mybir.dt.bfloat16
AF = mybir.ActivationFunctionType
ALU = mybir.AluOpType
AX = mybir.AxisListType

D_MODEL = 1024
BATCH = 4
SEQ = 8192
DEPTH = 4
D_FF = 2816
NFC = D_FF // 128
NDC = D_MODEL // 128
HYB_IN = 3080
TT = 512


class Buf:
    __slots__ = ("name", "lastw", "rd_c", "rd_d")

    def __init__(self, name):
        self.name = name
        self.lastw = None
        self.rd_c = {}
        self.rd_d = []


class Prog:
    ENGS = ("pe", "act", "dve", "pool", "sp")
    NDMASEM = 24

    def __init__(self, nc, es):
        self.nc = nc
        self.stage = 0
        self.ops = {e: [] for e in self.ENGS}
        self.ndma = {e: 0 for e in self.ENGS}
        self.dma_tok = {e: [] for e in self.ENGS}
        self.base = {e: 0 for e in self.ENGS}
        self.csem = {e: es.enter_context(nc.semaphore("cs_" + e)) for e in self.ENGS}
        self.dsem = {e: [es.enter_context(nc.semaphore("ds_%s_%d" % (e, i)))
                         for i in range(self.NDMASEM)] for e in ("sp", "pool")}
        self.eng = {"pe": nc.tensor, "act": nc.scalar, "dve": nc.vector,
                    "pool": nc.gpsimd, "sp": nc.sync}

    def _deps(self, reads, writes):
        cd = {}
        dd = []

        def add(tok):
            if tok is None or tok[1] != self.stage:
                return
            if tok[0] == "c":
                if cd.get(tok[2], -1) < tok[3]:
                    cd[tok[2]] = tok[3]
            else:
                dd.append(tok)
        for b in reads:
            add(b.lastw)
        for b in writes:
            add(b.lastw)
            for e, (st, i) in b.rd_c.items():
                add(("c", st, e, i))
            for t in b.rd_d:
                add(t)
        return cd, dd

    def _commit(self, tok, reads, writes):
        for b in reads:
            if tok[0] == "c":
                b.rd_c[tok[2]] = (tok[1], tok[3])
            else:
                b.rd_d = [t for t in b.rd_d if t[1] == self.stage][-8:] + [tok]
        for b in writes:
            b.lastw = tok
            b.rd_c = {}
            b.rd_d = []

    _cap = None

    def begin_capture(self):
        self._cap = [[]]

    def mark(self):
        if self._cap is not None:
            self._cap.append([])

    def end_capture(self):
        c, self._cap = self._cap, None
        return c

    def replay(self, item):
        if item[0] == "op":
            self.op(*item[1:])
        else:
            self.dma(*item[1:])

    def op(self, eng, fn, reads=(), writes=()):
        if self._cap is not None:
            self._cap[-1].append(("op", eng, fn, list(reads), list(writes)))
            return None
        cd, dd = self._deps(reads, writes)
        idx = len(self.ops[eng])
        tok = ("c", self.stage, eng, idx)
        self.ops[eng].append({"fn": fn, "cd": cd, "dd": dd, "dma": None})
        self._commit(tok, reads, writes)
        return tok

    def dma(self, q, out, in_, reads=(), writes=()):
        if self._cap is not None:
            self._cap[-1].append(("dma", q, out, in_, list(reads), list(writes)))
            return None
        cd, dd = self._deps(reads, writes)
        k = self.ndma[q]
        self.ndma[q] += 1
        slot = k % self.NDMASEM
        val = 16 * (k // self.NDMASEM + 1)
        if k >= self.NDMASEM:
            pt = self.dma_tok[q][k - self.NDMASEM]
            if pt[1] == self.stage:
                dd.append(pt)
        if q == "pool" and k >= 3:
            pt = self.dma_tok[q][k - 3]
            if pt[1] == self.stage:
                dd.append(pt)
        tok = ("d", self.stage, q, slot, val)
        self.dma_tok[q].append(tok)
        self.ops[q].append({"fn": None, "cd": cd, "dd": dd, "dma": (out, in_, slot)})
        self._commit(tok, reads, writes)
        return tok

    def emit(self):
        nc = self.nc
        need = {e: set() for e in self.ENGS}
        for e in self.ENGS:
            for o in self.ops[e]:
                for de, di in o["cd"].items():
                    if de == "pe" and e == "pe":
                        continue
                    need[de].add(di)
        cnt = {}
        for e in self.ENGS:
            c = self.base[e]
            arr = []
            for i in range(len(self.ops[e])):
                if i in need[e]:
                    c += 1
                arr.append(c)
            cnt[e] = arr
        final_waits = []
        for q in ("sp", "pool"):
            n = self.ndma[q]
            for s in range(min(n, self.NDMASEM)):
                last = (n - 1 - s) // self.NDMASEM
                final_waits.append((self.dsem[q][s], 16 * (last + 1)))
        with nc.Block() as block:
            def body(e):
                eng = self.eng[e]
                waited = {}
                for i, o in enumerate(self.ops[e]):
                    for de, di in o["cd"].items():
                        if de == "pe" and e == "pe":
                            continue
                        v = cnt[de][di]
                        if waited.get(de, 0) >= v:
                            continue
                        waited[de] = v
                        eng.wait_ge(self.csem[de], v)
                    for d in o["dd"]:
                        key = (d[2], d[3])
                        if waited.get(key, 0) >= d[4]:
                            continue
                        waited[key] = d[4]
                        eng.wait_ge(self.dsem[d[2]][d[3]], d[4])
                    if o["dma"] is not None:
                        out, in_, slot = o["dma"]
                        eng.dma_start(out=out, in_=in_).then_inc(self.dsem[e][slot], 16)
                    else:
                        inst = o["fn"](eng)
                        if i in need[e]:
                            inst.then_inc(self.csem[e], 1)
                if e == "sp":
                    for sem, v in final_waits:
                        eng.wait_ge(sem, v)

            @block.tensor
            def _(t):
                body("pe")

            @block.scalar
            def _(t):
                body("act")

            @block.vector
            def _(t):
                body("dve")

            @block.gpsimd
            def _(t):
                body("pool")

            @block.sync
            def _(t):
                body("sp")
        for e in self.ENGS:
            if cnt[e]:
                self.base[e] = cnt[e][-1]
            self.ops[e] = []
        self.stage += 1


class Ctx:
    pass


_UID = [0]


def U_(name):
    _UID[0] += 1
    return "%s_%d" % (name, _UID[0])


class Ring:
    def __init__(self, aps):
        self.aps = aps
        self.bufs = [Buf("r") for _ in aps]
        self.i = 0

    def next(self):
        k = self.i % len(self.aps)
        self.i += 1
        return self.aps[k], self.bufs[k]


def sb_ring(nc, es, name, n, shape, dt):
    t = es.enter_context(nc.sbuf_tensor(U_(name), [shape[0], n] + list(shape[1:]), dt))
    return Ring([t[:, i] for i in range(n)])


def run_steps(steps, depth=2):
    issued = 0
    handles = []
    n = len(steps)
    for i in range(n):
        while issued < min(n, i + depth + 1):
            lf = steps[issued][0]
            handles.append(lf() if lf is not None else None)
            issued += 1
        steps[i][1](handles[i])
        handles[i] = None


def t5_thresholds():
    d = np.arange(0, 256)
    max_exact = 16
    with np.errstate(divide="ignore"):
        large = max_exact + (np.log(np.maximum(d, 1).astype(np.float32) / np.float32(max_exact))
                             / np.float32(math.log(128 / max_exact)) * np.float32(32 - max_exact)).astype(np.int32)
    large = np.minimum(large, 31)
    bucket = np.where(d < max_exact, d, large)
    thr = [int(np.min(d[bucket >= b])) for b in range(32)]
    return thr


BV_LAYER = 920
BV_RB = 2 * BV_LAYER
BV_N = BV_RB + 128
PV_NORM = 0
PV_CW = 104
PV_CB = PV_CW + 64
PV_SCW = PV_CB + 16
PV_N = PV_SCW + 48


def lam_init_of(layer):
    return 0.8 - 0.6 * math.exp(-0.3 * layer)


class Builder:
    def __init__(self, NT, NP, ins, outs):
        self.NT, self.NP, self.NK = NT, NP, NT + NP
        self.ins, self.outs = set(ins), set(outs)
        self.nc = bass.Bass("TRN2", target_bir_lowering=False)
        self.es = ExitStack()
        self.P = Prog(self.nc, self.es)
        self.dr = {}
        self.dbuf = {}
        self.castbuf = {}

    def D(self, name, shape=None, dt=None):
        if name not in self.dr:
            kind = ("ExternalInput" if name in self.ins else
                    "ExternalOutput" if name in self.outs else "Internal")
            self.dr[name] = self.nc.dram_tensor(name, list(shape), dt, kind=kind).ap()
            self.dbuf[name] = Buf(name)
        return self.dr[name]

    def DB(self, name):
        return self.dbuf[name]

    def weight(self, name, shape):
        self.ins.add(name)
        src = self.D(name, shape, F32)
        dst = self.D(name + "_bf", shape, BF16)
        return src, dst

    def cast(self, name, idx, npieces=1, only=None):
        src, dst = self.dr[name], self.dr[name + "_bf"]
        s, d = src, dst
        for i in idx:
            s, d = s[i], d[i]
        n = s.shape[0]
        step = (n + npieces - 1) // npieces
        for pi, a in enumerate(range(0, n, step)):
            if only is not None and pi != only:
                continue
            b = Buf("cast")
            self.castbuf[(name, idx, pi)] = (b, a, min(n, a + step))
            self.P.dma("pool", d[a:min(n, a + step)], s[a:min(n, a + step)], writes=[b])

    def castbuf_for(self, name, idx, k):
        pi = 0
        while True:
            b, a, e = self.castbuf[(name, idx, pi)]
            if a <= k < e:
                return b
            pi += 1

    def tl_segment(self, x_src, x_dst, pre, ffns, post, tiles=None):
        nc, P, NT, NP = self.nc, self.P, self.NT, self.NP
        ntile = NT // TT
        import os
        with ExitStack() as es:
            def sb(name, shape, dt):
                return es.enter_context(nc.sbuf_tensor(U_(name), shape, dt))
            NXT = 2
            xts = [sb("xt", [128, 8, TT], F32) for _ in range(NXT)]
            b_xts = [[Buf("xt") for _ in range(8)] for _ in range(NXT)]
            cur = {}
            hT = sb("hT", [128, 8, TT], BF16)
            aT = sb("aT", [128, NFC, TT], BF16)
            ones = sb("ones", [128, 128], F32)
            cst = sb("cst", [128, 4], F32)
            pv = sb("pv", [128, PV_N], F32)
            rs = sb("rs", [128, TT], F32)
            rstd = sb("rstd", [128, TT], F32)
            b_ones, b_cst, b_pv, b_rs, b_rstd = [Buf(n) for n in range(5)]
            b_hT = [Buf("hT") for _ in range(8)]
            b_aT = [Buf("aT") for _ in range(NFC)]
            sq = sb_ring(nc, es, "sq", 2, [128, TT], F32)
            sil = sb_ring(nc, es, "sil", 2, [128, TT], F32)
            gu = sb_ring(nc, es, "gu", 3, [128, 2, 2, 8, 128], BF16)
            wdr = sb_ring(nc, es, "wdr", 3, [128, NFC, 128], BF16)
            w1 = sb_ring(nc, es, "w1", 9 if post[0] == "odd" else 6, [128, 8, 128], BF16)
            s32 = sb_ring(nc, es, "s32", 4, [128, TT], F32)
            s16 = sb_ring(nc, es, "s16", 4, [128, TT], BF16)
            banks = Ring([es.enter_context(nc.psum_tensor(U_("bk"), [128, 512], F32))[:]
                          for i in range(7)])
            nbank = es.enter_context(nc.psum_tensor(U_("nbk"), [128, 512], F32))[:]
            b_nb = Buf("nbank")
            if pre is not None:
                mixs = [sb("mix", [128, 8, TT], BF16) for _ in range(2)]
                b_mixs = [Buf("mix") for _ in range(2)]
            if pre is not None and pre[0] == "odd":
                vvr = sb_ring(nc, es, "vvr", 3, [128, TT + 2], F32)
                bgr = sb_ring(nc, es, "bgr", 3, [128, TT], F32)
                cacc = sb_ring(nc, es, "cacc", 2, [128, TT], F32)
            if post[0] == "even":
                w8k = sb_ring(nc, es, "w8k", 1, [128, 8, 512], BF16)
                wdt = sb("wdt", [128, 8, 8], BF16)
                dts = sb("dts", [128, 4, 8], F32)
                b_wdt, b_dts = Buf("wdt"), Buf("dts")

            P.op("dve", lambda e: e.memset(ones[:], 1.0), writes=[b_ones])
            P.op("dve", lambda e: e.memset(cst[:, 0:1], 1e-6), writes=[b_cst])
            P.op("dve", lambda e: e.memset(cst[:, 1:2], 1e-5), writes=[b_cst])
            P.op("dve", lambda e: e.memset(cst[:, 2:3], 1.0), writes=[b_cst])
            P.op("dve", lambda e: e.memset(cst[:, 3:4], 0.0), writes=[b_cst])
            P.dma("sp", pv[:], self.dr["pvec"], writes=[b_pv])
            if post[0] == "even":
                j = post[1]
                P.dma("sp", wdt[:], self.dr["hwin_t_bf"][j, 0][:, :, 1024:1032],
                      reads=[self.castbuf_for("hwin_t", (j,), 0)], writes=[b_wdt])

            def norm_sq(c):
                xt, b_xt = cur["xt"], cur["bx"]
                s, sbf = sq.next()
                P.op("act", lambda e: e.activation(out=s, in_=xt[:, c, :], func=AF.Square),
                     reads=[b_xt[c]], writes=[sbf])
                return s, sbf

            def norm_mm(c, ssb):
                s, sbf = ssb
                P.op("pe", lambda e: e.matmul(nbank, lhsT=ones[:], rhs=s, start=(c == 0), stop=(c == 7)),
                     reads=[sbf, b_ones], writes=[b_nb])

            def norm_fin(nspec):
                xt, b_xt = cur["xt"], cur["bx"]
                nidx, final, t0 = nspec
                P.op("act", lambda e: e.activation(out=rs[:], in_=nbank, func=AF.Sqrt, bias=cst[:, 0:1], scale=1.0 / 1024),
                     reads=[b_nb, b_cst], writes=[b_rs])
                P.op("dve", lambda e: e.reciprocal(out=rstd[:], in_=rs[:]), reads=[b_rs], writes=[b_rstd])
                for c in range(8):
                    col = PV_NORM + nidx * 8 + c
                    if not final:
                        P.op("dve", lambda e, c=c, col=col: e.scalar_tensor_tensor(
                            out=hT[:, c, :], in0=xt[:, c, :], scalar=pv[:, col:col + 1], in1=rstd[:],
                            op0=ALU.mult, op1=ALU.mult), reads=[b_xt[c], b_rstd, b_pv], writes=[b_hT[c]])
                    else:
                        s, sbf = s32.next()
                        P.op("dve", lambda e, c=c, col=col, s=s: e.scalar_tensor_tensor(
                            out=s, in0=xt[:, c, :], scalar=pv[:, col:col + 1], in1=rstd[:],
                            op0=ALU.mult, op1=ALU.mult), reads=[b_xt[c], b_rstd, b_pv], writes=[sbf])
                        P.dma("sp", self.dr["outT"][c * 128:(c + 1) * 128, t0:t0 + TT], s, reads=[sbf],
                              writes=[self.DB("outT")])

            def norm_full(nspec):
                for c in range(8):
                    norm_mm(c, norm_sq(c))
                norm_fin(nspec)

            class Upd:
                def __init__(self, nspec):
                    self.nspec = nspec
                    self.pend = None

                def after_mm(self):
                    if self.pend is not None:
                        norm_mm(*self.pend)
                        self.pend = None

                def after_update(self, d):
                    if self.nspec is None:
                        return
                    self.pend = (d, norm_sq(d))
                    if d == 7:
                        norm_mm(*self.pend)
                        self.pend = None
                        norm_fin(self.nspec)

            def ffn_steps(fi, nspec):
                steps = []
                wgu_bf, wd_bf = self.dr["wgu_bf"], self.dr["wd_bf"]
                upd = Upd(nspec)
                for f2 in range(0, NFC, 2):
                    def loads(f2=f2):
                        slot, sbf = gu.next()
                        for g in range(2):
                            P.dma("sp", slot[:, g], wgu_bf[fi, g, f2:f2 + 2].rearrange("f p k j -> p f k j"),
                                  reads=[self.castbuf_for("wgu", (fi, g), f2), self.castbuf_for("wgu", (fi, g), f2 + 1)],
                                  writes=[sbf])
                        return slot, sbf

                    def comp(h, f2=f2):
                        slot, sbf = h
                        for ff in range(2):
                            f = f2 + ff
                            pg, bg = banks.next()
                            pu, bu = banks.next()
                            for kc in range(8):
                                P.op("pe", lambda e, kc=kc, ff=ff, pg=pg: e.matmul(pg, lhsT=slot[:, 0, ff, kc, :], rhs=hT[:, kc, :],
                                                                                   start=(kc == 0), stop=(kc == 7)),
                                     reads=[sbf, b_hT[kc]], writes=[bg])
                            for kc in range(8):
                                P.op("pe", lambda e, kc=kc, ff=ff, pu=pu: e.matmul(pu, lhsT=slot[:, 1, ff, kc, :], rhs=hT[:, kc, :],
                                                                                   start=(kc == 0), stop=(kc == 7)),
                                     reads=[sbf, b_hT[kc]], writes=[bu])
                            s_, ssb = sil.next()
                            P.op("act", lambda e, s_=s_, pg=pg: e.activation(out=s_, in_=pg, func=AF.Silu), reads=[bg], writes=[ssb])
                            P.op("dve", lambda e, s_=s_, pu=pu, f=f: e.tensor_tensor(out=aT[:, f, :], in0=pu, in1=s_, op=ALU.mult),
                                 reads=[bu, ssb], writes=[b_aT[f]])
                    steps.append((loads, comp))
                for d in range(8):
                    def loads(d=d):
                        slot, sbf = wdr.next()
                        P.dma("sp", slot, wd_bf[fi, d], reads=[self.castbuf_for("wd", (fi,), d)], writes=[sbf])
                        return slot, sbf

                    def comp(h, d=d):
                        xt, b_xt = cur["xt"], cur["bx"]
                        slot, sbf = h
                        py, by = banks.next()
                        for fc in range(NFC):
                            P.op("pe", lambda e, fc=fc: e.matmul(py, lhsT=slot[:, fc, :], rhs=aT[:, fc, :],
                                                                  start=(fc == 0), stop=(fc == NFC - 1)),
                                 reads=[sbf, b_aT[fc]], writes=[by])
                        upd.after_mm()
                        P.op("dve", lambda e: e.scalar_tensor_tensor(out=xt[:, d, :], in0=py, scalar=0.5, in1=xt[:, d, :],
                                                                     op0=ALU.mult, op1=ALU.add),
                             reads=[by, b_xt[d]], writes=[b_xt[d]])
                        upd.after_update(d)
                    steps.append((loads, comp))
                return steps

            def proj_fm_group(wname, j, items):
                w_bf = self.dr[wname + "_bf"]

                def loads():
                    hs = []
                    for ci, _ in items:
                        slot, sbf = w1.next()
                        P.dma("sp", slot, w_bf[j, ci], reads=[self.castbuf_for(wname, (j,), ci)], writes=[sbf])
                        hs.append((slot, sbf))
                    return hs

                def comp(h):
                    for (slot, sbf), (ci, evac) in zip(h, items):
                        bank, bb = banks.next()
                        for kc in range(8):
                            P.op("pe", lambda e, kc=kc, slot=slot, bank=bank: e.matmul(bank, lhsT=slot[:, kc, :], rhs=hT[:, kc, :],
                                                                                       start=(kc == 0), stop=(kc == 7)),
                                 reads=[sbf, b_hT[kc]], writes=[bb])
                        evac(bank, bb)
                return (loads, comp)

            def outproj_steps(wname, j, nspec):
                steps = []
                w_bf = self.dr[wname + "_bf"]
                upd = Upd(nspec)
                for d2 in range(0, 8, 2):
                    def loads(d2=d2):
                        hs = []
                        for d in (d2, d2 + 1):
                            slot, sbf = w1.next()
                            P.dma("sp", slot, w_bf[j, d], reads=[self.castbuf_for(wname, (j,), d)], writes=[sbf])
                            hs.append((slot, sbf))
                        return hs

                    def comp(h, d2=d2):
                        xt, b_xt = cur["xt"], cur["bx"]
                        mix, b_mix = cur["mix"], cur["bm"]
                        for (slot, sbf), d in zip(h, (d2, d2 + 1)):
                            bank, bb = banks.next()
                            for mc in range(8):
                                P.op("pe", lambda e, mc=mc, slot=slot, bank=bank: e.matmul(bank, lhsT=slot[:, mc, :], rhs=mix[:, mc, :],
                                                                                           start=(mc == 0), stop=(mc == 7)),
                                     reads=[sbf, b_mix], writes=[bb])
                            upd.after_mm()
                            P.op("dve", lambda e, bank=bank, d=d: e.tensor_tensor(out=xt[:, d, :], in0=bank, in1=xt[:, d, :], op=ALU.add),
                                 reads=[bb, b_xt[d]], writes=[b_xt[d]])
                            upd.after_update(d)
                    steps.append((loads, comp))
                return steps

            steps = []
            xs_ap = self.dr[x_src].rearrange("(c p) t -> p c t", p=128)
            tlist = list(range(ntile) if tiles is None else tiles)
            mx = self.dr["mixT"].rearrange("(c p) t -> p c t", p=128) if (pre is not None and pre[0] == "even") else None

            def conv_steps(ti):
                j = pre[1]
                t0_ = tlist[ti] * TT
                mix, b_mix = mixs[ti % 2], b_mixs[ti % 2]
                vv = self.dr["vvT"]
                bgd = self.dr["bgT"]
                out = []
                for c in range(8):
                    def ld_v(c=c):
                        v_, vb = vvr.next()
                        g_, gb = bgr.next()
                        P.dma("sp", v_, vv[c * 128:(c + 1) * 128, t0_:t0_ + TT + 2], reads=[self.DB("vvT")], writes=[vb])
                        P.dma("sp", g_, bgd[c * 128:(c + 1) * 128, t0_:t0_ + TT], reads=[self.DB("bgT")], writes=[gb])
                        return v_, vb, g_, gb

                    def conv(h, c=c):
                        v_, vb, g_, gb = h
                        a, ab = cacc.next()
                        wc = PV_SCW + (j * 8 + c) * 3
                        P.op("dve", lambda e: e.tensor_scalar(
                            out=a, in0=v_[:, 2:TT + 2], scalar1=pv[:, wc + 2:wc + 3], scalar2=None,
                            op0=ALU.mult), reads=[vb, b_pv], writes=[ab])
                        P.op("dve", lambda e: e.scalar_tensor_tensor(
                            out=a, in0=v_[:, 1:TT + 1], scalar=pv[:, wc + 1:wc + 2], in1=a,
                            op0=ALU.mult, op1=ALU.add), reads=[vb, b_pv, ab], writes=[ab])
                        P.op("dve", lambda e: e.scalar_tensor_tensor(
                            out=a, in0=v_[:, 0:TT], scalar=pv[:, wc:wc + 1], in1=a,
                            op0=ALU.mult, op1=ALU.add), reads=[vb, b_pv, ab], writes=[ab])
                        P.op("dve", lambda e: e.tensor_tensor(
                            out=mix[:, c, :], in0=a, in1=g_, op=ALU.mult),
                            reads=[ab, gb], writes=[b_mix])
                    out.append((ld_v, conv))
                return out

            def prefetch(ti):
                if ti >= len(tlist):
                    return
                t0_ = tlist[ti] * TT
                P.dma("sp", xts[ti % NXT][:], xs_ap[:, :, t0_:t0_ + TT], reads=[self.DB(x_src)], writes=b_xts[ti % NXT])
                if mx is not None:
                    P.dma("sp", mixs[ti % 2][:], mx[:, :, t0_:t0_ + TT], reads=[self.DB("mixT")], writes=[b_mixs[ti % 2]])
            steps.append((None, lambda h: prefetch(0)))
            for ti, t in enumerate(tlist):
                t0 = t * TT
                xslot, bxs = xts[ti % NXT], b_xts[ti % NXT]

                def set_cur(h, xslot=xslot, bxs=bxs, ti=ti):
                    cur["xt"], cur["bx"] = xslot, bxs
                    if pre is not None:
                        cur["mix"], cur["bm"] = mixs[ti % 2], b_mixs[ti % 2]
                steps.append((None, set_cur))
                tile_first_step = len(steps)
                nlist = [(nidx, False, t0) for (_, nidx) in ffns]
                if post[0] == "final":
                    nlist.append((post[1], True, t0))
                else:
                    nlist.append((post[2], False, t0))
                if pre is not None and pre[0] == "even":
                    steps += outproj_steps("hwout", pre[1], nlist[0])
                elif pre is not None:
                    if ti == 0:
                        steps += conv_steps(0)
                    steps += outproj_steps("scwout", pre[1], nlist[0])
                else:
                    steps.append((None, lambda h, ns=nlist[0]: norm_full(ns)))
                for k, (fi, nidx) in enumerate(ffns):
                    steps += ffn_steps(fi, nlist[k + 1])
                if post[0] == "final":
                    pass
                elif post[0] == "odd":
                    j = post[1]
                    for c in range(8):
                        hold = {}

                        def ev_cg(bank, bb, hold=hold):
                            s, sbf = s32.next()
                            hold["cg"] = (s, sbf)
                            P.op("act", lambda e: e.activation(out=s, in_=bank, func=AF.Copy), reads=[bb], writes=[sbf])

                        def ev_u(bank, bb, hold=hold, c=c, t0=t0):
                            s, sbf = hold["cg"]
                            o, obf = s32.next()
                            P.op("dve", lambda e: e.tensor_tensor(out=o, in0=bank, in1=s, op=ALU.mult),
                                 reads=[bb, sbf], writes=[obf])
                            P.dma("sp", self.dr["vvT"][c * 128:(c + 1) * 128, 2 + t0:2 + t0 + TT], o,
                                  reads=[obf], writes=[self.DB("vvT")])

                        def ev_bg(bank, bb, c=c, t0=t0):
                            s, sbf = s32.next()
                            P.op("act", lambda e: e.activation(out=s, in_=bank, func=AF.Copy), reads=[bb], writes=[sbf])
                            P.dma("sp", self.dr["bgT"][c * 128:(c + 1) * 128, t0:t0 + TT], s,
                                  reads=[sbf], writes=[self.DB("bgT")])
                        steps.append(proj_fm_group("scwin", j, [(8 + c, ev_cg), (16 + c, ev_u), (c, ev_bg)]))
                else:
                    j, NPo = post[1], post[3]
                    hw_t = self.dr["hwin_t_bf"]

                    def vz_step(part, t0=t0):
                        def loads():
                            slot, sbf = w8k.next()
                            P.dma("sp", slot, hw_t[j, 0][:, :, part * 512:(part + 1) * 512],
                                  reads=[self.castbuf_for("hwin_t", (j,), 0)], writes=[sbf])
                            return slot, sbf

                        def comp(h):
                            slot, sbf = h
                            for sbi in range(TT // 128):
                                bank, bb = banks.next()
                                for kc in range(8):
                                    P.op("pe", lambda e, kc=kc, sbi=sbi, bank=bank: e.matmul(
                                        bank, lhsT=hT[:, kc, sbi * 128:(sbi + 1) * 128], rhs=slot[:, kc, :],
                                        start=(kc == 0), stop=(kc == 7)), reads=[sbf, b_hT[kc]], writes=[bb])
                                r0 = t0 + sbi * 128
                                if part == 0:
                                    s, s_b = s16.next()
                                    P.op("act", lambda e, s=s, bank=bank: e.activation(out=s, in_=bank, func=AF.Copy), reads=[bb], writes=[s_b])
                                    P.dma("sp", self.dr["v"][NPo + r0:NPo + r0 + 128, :], s, reads=[s_b], writes=[self.DB("v")])
                                else:
                                    s, s_b = s32.next()
                                    P.op("dve", lambda e, s=s, bank=bank: e.tensor_copy(out=s, in_=bank), reads=[bb], writes=[s_b])
                                    P.dma("sp", self.dr["z"][r0:r0 + 128, :], s, reads=[s_b], writes=[self.DB("z")])
                        return (loads, comp)
                    steps.append(vz_step(0))
                    evs = []
                    for ci in range(16):
                        def ev(bank, bb, ci=ci, t0=t0):
                            if ci < 8:
                                s, sbf = s16.next()
                                name, r0, c0 = ("qT", ci * 128, t0) if ci < 4 else ("kT", (ci - 4) * 128, NPo + t0)
                            else:
                                s, sbf = s32.next()
                                name, r0, c0 = "xbcT", (ci - 8) * 128, NPo + t0
                            if ci % 2 == 0:
                                P.op("act", lambda e: e.activation(out=s, in_=bank, func=AF.Copy), reads=[bb], writes=[sbf])
                            else:
                                P.op("dve", lambda e: e.tensor_copy(out=s, in_=bank), reads=[bb], writes=[sbf])
                            P.dma("sp", self.dr[name][r0:r0 + 128, c0:c0 + TT], s, reads=[sbf], writes=[self.DB(name)])
                        evs.append((ci, ev))
                    for g0 in range(0, 16, 2):
                        steps.append(proj_fm_group("hwin_f", j, evs[g0:g0 + 2]))
                    steps.append(vz_step(1))

                    def comp_dt(h, t0=t0):
                        bank, bb = banks.next()
                        for sbi in range(TT // 128):
                            for kc in range(8):
                                P.op("pe", lambda e, kc=kc, sbi=sbi: e.matmul(
                                    bank[:, sbi * 8:(sbi + 1) * 8], lhsT=hT[:, kc, sbi * 128:(sbi + 1) * 128],
                                    rhs=wdt[:, kc, :], start=(kc == 0), stop=(kc == 7)),
                                    reads=[b_wdt, b_hT[kc]], writes=[bb])
                        P.op("dve", lambda e: e.tensor_copy(out=dts[:].rearrange("p a b -> p (a b)"), in_=bank[:, 0:32]),
                             reads=[bb], writes=[b_dts])
                        P.dma("sp", self.dr["dt"][NPo + t0:NPo + t0 + TT, :].rearrange("(s p) h -> p s h", p=128),
                              dts[:], reads=[b_dts], writes=[self.DB("dt")])
                    steps.append((None, comp_dt))
                ppos = min(len(steps), tile_first_step + 12)
                steps.insert(ppos, (None, lambda h, ti=ti: prefetch(ti + 1)))
                if pre is not None and pre[0] == "odd" and ti + 1 < len(tlist):
                    steps[ppos + 1:ppos + 1] = conv_steps(ti + 1)
                if post[0] != "final":
                    xd_ap = self.dr[x_dst].rearrange("(c p) t -> p c t", p=128)

                    def st_x(h, t0=t0, xslot=xslot, bxs=bxs):
                        P.dma("sp", xd_ap[:, :, t0:t0 + TT], xslot[:], reads=bxs, writes=[self.DB(x_dst)])
                    steps.append((None, st_x))
            if post[0] == "odd" and NP == 0:
                zs, zsb = s32.next()
                P.op("dve", lambda e: e.memset(zs[:, 0:16], 0.0), writes=[zsb])
                P.dma("sp", self.dr["vvT"].rearrange("(c p) t -> p c t", p=128)[:, :, 0:2],
                      zs[:, 0:16].rearrange("p (c t) -> p c t", c=8), reads=[zsb], writes=[self.DB("vvT")])
            run_steps(steps, depth=2)
            P.emit()


def host_weights(inp):
    f = lambda a: np.ascontiguousarray(np.asarray(a, dtype=np.float32))
    W = {}
    wg = np.stack([np.stack([f(inp["ffn1_wg"])[l], f(inp["ffn2_wg"])[l]]) for l in range(DEPTH)]).reshape(8, 1024, 2816)
    wu = np.stack([np.stack([f(inp["ffn1_wu"])[l], f(inp["ffn2_wu"])[l]]) for l in range(DEPTH)]).reshape(8, 1024, 2816)
    wdn = np.stack([np.stack([f(inp["ffn1_wd"])[l], f(inp["ffn2_wd"])[l]]) for l in range(DEPTH)]).reshape(8, 2816, 1024)

    def fm(w):
        lead = w.shape[:-2]
        K, F = w.shape[-2:]
        w = w.reshape(lead + (K // 128, 128, F // 128, 128))
        nd = len(lead)
        return np.ascontiguousarray(np.transpose(w, tuple(range(nd)) + (nd + 2, nd + 1, nd, nd + 3)))
    W["wgu"] = np.ascontiguousarray(np.stack([fm(wg), fm(wu)], axis=1))
    W["wd"] = fm(wdn)
    hw = f(inp["hyb_w_in"])
    cols_f = np.concatenate([np.arange(0, 1024), np.arange(2048, 3072)])
    cols_t = np.concatenate([np.arange(1024, 2048), np.arange(3072, 3080)])
    W["hwin_f"] = fm(hw[:, :, cols_f])
    wt = hw[:, :, cols_t].reshape(2, 8, 128, 1032)
    W["hwin_t"] = np.ascontiguousarray(np.transpose(wt, (0, 2, 1, 3)))[:, None]
    W["hwout"] = fm(f(inp["hyb_w_out"]))
    W["scwin"] = fm(f(inp["sc_w_in"]))
    W["scwout"] = fm(f(inp["sc_w_out"]))
    pv = np.zeros((128, PV_N), np.float32)
    norms = np.zeros((13, 1024), np.float32)
    for l in range(DEPTH):
        norms[3 * l + 0] = f(inp["ffn1_norm"])[l]
        norms[3 * l + 1] = f(inp["mix_norm"])[l]
        norms[3 * l + 2] = f(inp["ffn2_norm"])[l]
    norms[12] = f(inp["final_norm_w"])
    pv[:, PV_NORM:PV_NORM + 104] = norms.reshape(13, 8, 128).transpose(2, 0, 1).reshape(128, 104)
    cw = f(inp["ssm_conv_w"]).reshape(2, 4, 8, 128)
    pv[:, PV_CW:PV_CW + 64] = cw.transpose(3, 0, 2, 1).reshape(128, 64)
    cb = f(inp["ssm_conv_b"]).reshape(2, 8, 128)
    pv[:, PV_CB:PV_CB + 16] = cb.transpose(2, 0, 1).reshape(128, 16)
    scw = f(inp["sc_conv_w"]).reshape(2, 3, 8, 128)
    pv[:, PV_SCW:PV_SCW + 48] = scw.transpose(3, 0, 2, 1).reshape(128, 48)
    W["pvec"] = pv
    bv = np.zeros((1, BV_N), np.float32)
    for j in range(2):
        o = j * BV_LAYER
        bv[0, o:o + 64] = f(inp["diff_lq1"])[j]
        bv[0, o + 64:o + 128] = f(inp["diff_lk1"])[j]
        bv[0, o + 128:o + 192] = f(inp["diff_lq2"])[j]
        bv[0, o + 192:o + 256] = f(inp["diff_lk2"])[j]
        bv[0, o + 256:o + 384] = f(inp["diff_subln_w"])[j]
        bv[0, o + 384:o + 392] = f(inp["ssm_dt_bias"])[j]
        bv[0, o + 392:o + 400] = f(inp["ssm_a_log"])[j]
        bv[0, o + 400:o + 408] = f(inp["ssm_d"])[j]
        bv[0, o + 408:o + 920] = f(inp["ssm_norm_w"])[j]
    bv[0, BV_RB:BV_RB + 128] = f(inp["rel_bias"]).reshape(128)
    W["bvec"] = bv
    return W


WSHAPES = {"wgu": [8, 2, 22, 128, 8, 128], "wd": [8, 8, 128, 22, 128],
           "hwin_f": [2, 16, 128, 8, 128], "hwin_t": [2, 1, 128, 8, 1032],
           "hwout": [2, 8, 128, 8, 128], "scwin": [2, 24, 128, 8, 128],
           "scwout": [2, 8, 128, 8, 128]}


def declare_common(B, used):
    for n in used:
        B.weight(n, WSHAPES[n])
    B.ins.add("pvec")
    B.ins.add("bvec")
    B.D("pvec", [128, PV_N], F32)
    B.D("bvec", [1, BV_N], F32)


def _bc(ap, shape):
    return ap.to_broadcast(shape)


class Stage:
    def __init__(self, B, es):
        self.B, self.nc, self.P, self.es = B, B.nc, B.P, es

    def sb(self, name, shape, dt):
        return self.es.enter_context(self.nc.sbuf_tensor(U_(name), shape, dt))

    def banks(self, n=8):
        return Ring([self.es.enter_context(self.nc.psum_tensor(U_("bk"), [128, 512], F32))[:]
                     for i in range(n)])


def ssd_stage(B, j):
    nc, P, NT, NP, NK = B.nc, B.P, B.NT, B.NP, B.NK
    with ExitStack() as es:
        S = Stage(B, es)
        sb = S.sb
        banks = S.banks()
        xbt = sb("xbt", [128, 8, 515], F32); b_xbt = Buf("xbt")
        cacc = sb_ring(nc, es, "cacc", 2, [128, 512], F32)
        xsT = sb("xsT", [128, 4, 512], F32); b_xsT = [Buf("xsT") for _ in range(4)]
        bcT = sb("bcT", [128, 4, 512], BF16); b_bcT = [Buf("bcT") for _ in range(4)]
        xtok = sb("xtok", [128, 4, 512], F32); b_xtok = [Buf("xtok") for _ in range(4)]
        btok = sb("btok", [128, 4, 2, 128], BF16); b_btok = [Buf("btok") for _ in range(4)]
        dtr = sb("dtr", [128, 4, 8], F32); b_dtr = Buf("dtr")
        zt = sb("zt", [128, 4, 512], F32); b_zt = Buf("zt")
        bv = sb("bv", [128, BV_LAYER], F32); b_bv = Buf("bv")
        pv = sb("pv", [128, PV_N], F32); b_pv = Buf("pv")
        flag = sb("flag", [128, 1], F32); b_flag = Buf("flag")
        sm = sb("sm", [128, 24, 32], F32)
        b_sm = [Buf("sm%d" % i) for i in range(24)]
        U = sb("U", [128, 128], F32); ones = sb("ones", [128, 128], F32)
        idf = sb("idf", [128, 128], F32); idb = sb("idb", [128, 128], BF16)
        mneg = sb("mneg", [128, 4, 128], F32)
        b_const = Buf("const")
        dtile = sb("dtile", [128, 8, 64], F32); nwt = sb("nwt", [128, 512], F32)
        cst = sb("cst", [128, 4], F32)
        R_2 = [sb("R", [128, 8, 128], F32) for _ in range(2)]; b_R2 = [Buf("R") for _ in range(2)]
        Abm2 = [sb("Abm", [128, 8, 128], F32) for _ in range(2)]; b_Abm2 = [Buf("Abm") for _ in range(2)]
        eAB2 = [sb("eAB", [128, 8, 128], F32) for _ in range(2)]; b_eAB2 = [Buf("eAB") for _ in range(2)]
        L_2 = [sb("L", [128, 8, 128], F32) for _ in range(2)]; b_L2 = [Buf("L") for _ in range(2)]
        M_2 = [sb("M", [128, 8, 128], BF16) for _ in range(2)]; b_M2 = [Buf("M") for _ in range(2)]
        CsT2 = [sb("CsT", [128, 8, 128], BF16) for _ in range(2)]; b_CsT2 = [Buf("CsT") for _ in range(2)]
        X_2 = [sb("X", [128, 8, 64], BF16) for _ in range(2)]; b_X2 = [Buf("X") for _ in range(2)]
        Xd2 = [sb("Xd", [128, 8, 64], BF16) for _ in range(2)]; b_Xd2 = [Buf("Xd") for _ in range(2)]
        St = sb("St", [128, 8, 64], F32); b_St = Buf("St")
        Stmp = sb("Stmp", [128, 8, 64], F32); b_Stmp = Buf("Stmp")
        Sbf = sb("Sbf", [128, 8, 64], BF16); b_Sbf = Buf("Sbf")
        ysb2 = [sb("ysb", [128, 512], F32) for _ in range(2)]; b_ysb2 = [Buf("ysb") for _ in range(2)]
        ytmp2 = [sb("ytmp", [128, 512], F32) for _ in range(2)]; b_ytmp2 = [Buf("ytmp") for _ in range(2)]
        szt2 = [sb("szt", [128, 512], F32) for _ in range(2)]; b_szt2 = [Buf("szt") for _ in range(2)]
        junk2 = [sb("junk", [128, 256], F32) for _ in range(2)]; b_junk2 = [Buf("junk") for _ in range(2)]
        gn2 = [sb("gn", [128, 512], BF16) for _ in range(2)]; b_gn2 = [Buf("gn") for _ in range(2)]
        ymT = sb("ymT", [128, 4, 512], BF16); b_ymT = Buf("ymT")

        o = j * BV_LAYER
        P.dma("sp", bv[:], B.dr["bvec"][0:1, o:o + BV_LAYER].partition_broadcast(128), writes=[b_bv])
        P.dma("sp", pv[:], B.dr["pvec"], writes=[b_pv])
        P.dma("sp", flag[:], B.dr["flag"], writes=[b_flag])
        V = lambda f, **kw: P.op("dve", f, **kw)
        V(lambda e: e.memset(ones[:], 1.0), writes=[b_const])
        V(lambda e: e.memset(U[:], 1.0), writes=[b_const])
        V(lambda e: e.memset(idf[:], 1.0), writes=[b_const])
        V(lambda e: e.memset(mneg[:], 0.0), writes=[b_const])
        V(lambda e: e.memset(St[:], 0.0), writes=[b_St])
        V(lambda e: e.memset(Sbf[:], 0.0), writes=[b_Sbf])
        V(lambda e: e.memset(cst[:, 0:1], 1e-5), writes=[b_const])
        V(lambda e: e.memset(cst[:, 1:2], 1.0), writes=[b_const])
        P.op("pool", lambda e: e.affine_select(out=U[:], in_=U[:], pattern=[[1, 128]], compare_op=ALU.is_ge,
                                               fill=0.0, base=0, channel_multiplier=-1),
             reads=[b_const], writes=[b_const])
        P.op("pool", lambda e: e.affine_select(out=idf[:], in_=idf[:], pattern=[[1, 128]], compare_op=ALU.is_equal,
                                               fill=0.0, base=0, channel_multiplier=-1),
             reads=[b_const], writes=[b_const])
        for r in range(4):
            P.op("pool", lambda e, r=r: e.affine_select(out=mneg[:, r, :], in_=mneg[:, r, :], pattern=[[1, 128]],
                                                        compare_op=ALU.is_ge, fill=-1e30, base=0,
                                                        channel_multiplier=-1),
                 reads=[b_const], writes=[b_const])
        V(lambda e: e.tensor_copy(out=idb[:], in_=idf[:]), reads=[b_const], writes=[b_const])
        dtb = bv[:, 384:392]
        V(lambda e: e.tensor_copy(out=dtile[:], in_=_bc(bv[:, 400:408].unsqueeze(2), [128, 8, 64])),
          reads=[b_bv], writes=[b_const])
        V(lambda e: e.tensor_copy(out=nwt[:], in_=bv[:, 408:920]), reads=[b_bv], writes=[b_const])
        aneg = sm[:, 0, 0:8]
        P.op("act", lambda e: e.activation(out=aneg, in_=bv[:, 392:400], func=AF.Exp), reads=[b_bv], writes=[b_sm[0]])
        V(lambda e: e.tensor_scalar(out=aneg, in0=aneg, scalar1=-1.0, scalar2=None, op0=ALU.mult),
          reads=[b_sm[0]], writes=[b_sm[0]])

        xb_d = B.dr["xbcT"].rearrange("(c p) t -> p c t", p=128)
        nsb = NK // 512
        for sk in range(nsb):
            own = sk * 512 >= NP
            nch = 8 if own else 6
            c0 = sk * 512
            if sk == 0:
                V(lambda e: e.memset(xbt[:, :, 0:3], 0.0), writes=[b_xbt])
                P.dma("sp", xbt[:, 0:nch, 3:515], xb_d[:, 0:nch, 0:512], reads=[B.DB("xbcT")], writes=[b_xbt])
            else:
                P.dma("sp", xbt[:, 0:nch, :], xb_d[:, 0:nch, c0 - 3:c0 + 512], reads=[B.DB("xbcT")], writes=[b_xbt])
            P.dma("sp", dtr[:], B.dr["dt"][c0:c0 + 512, :].rearrange("(s p) h -> p s h", p=128),
                  reads=[B.DB("dt")], writes=[b_dtr])
            if own:
                P.dma("sp", zt[:], B.dr["z"][c0 - NP:c0 - NP + 512, :].rearrange("(s p) f -> p s f", p=128),
                      reads=[B.DB("z")], writes=[b_zt])
            for c in range(nch):
                a, ab = cacc.next()
                wc = PV_CW + (j * 8 + c) * 4
                bc_ = PV_CB + j * 8 + c
                V(lambda e, c=c, a=a, wc=wc, bc_=bc_: e.tensor_scalar(
                    out=a, in0=xbt[:, c, 3:515], scalar1=pv[:, wc + 3:wc + 4], scalar2=pv[:, bc_:bc_ + 1],
                    op0=ALU.mult, op1=ALU.add), reads=[b_xbt, b_pv], writes=[ab])
                for k in range(3):
                    V(lambda e, c=c, a=a, wc=wc, k=k: e.scalar_tensor_tensor(
                        out=a, in0=xbt[:, c, k:k + 512], scalar=pv[:, wc + k:wc + k + 1], in1=a,
                        op0=ALU.mult, op1=ALU.add), reads=[b_xbt, b_pv, ab], writes=[ab])
                if c < 4:
                    P.op("act", lambda e, c=c, a=a: e.activation(out=xsT[:, c, :], in_=a, func=AF.Silu),
                         reads=[ab], writes=[b_xsT[c]])
                else:
                    P.op("act", lambda e, c=c, a=a: e.activation(out=bcT[:, c - 4, :], in_=a, func=AF.Silu),
                         reads=[ab], writes=[b_bcT[c - 4]])
            xr, ax, ee, dtt, dA = [sm[:, i, :].rearrange("p (a b) -> p a b", a=4) for i in (1, 2, 3, 4, 5)]
            V(lambda e: e.tensor_tensor(out=xr, in0=dtr[:], in1=_bc(dtb.unsqueeze(1), [128, 4, 8]), op=ALU.add),
              reads=[b_dtr, b_bv], writes=[b_sm[1]])
            V(lambda e: e.scalar_tensor_tensor(out=ax, in0=xr, scalar=-1.0, in1=xr, op0=ALU.mult, op1=ALU.max), reads=[b_sm[1]], writes=[b_sm[2]])
            P.op("act", lambda e: e.activation(out=ee, in_=ax, func=AF.Exp, scale=-1.0), reads=[b_sm[2]], writes=[b_sm[3]])
            P.op("act", lambda e: e.activation(out=ee, in_=ee, func=AF.Ln, bias=cst[:, 1:2], scale=1.0),
                 reads=[b_sm[3], b_const], writes=[b_sm[3]])
            V(lambda e: e.scalar_tensor_tensor(out=dtt, in0=xr, scalar=0.0, in1=ee, op0=ALU.max, op1=ALU.add),
              reads=[b_sm[1], b_sm[3]], writes=[b_sm[4]])
            V(lambda e: e.tensor_tensor(out=dA, in0=dtt, in1=_bc(aneg.unsqueeze(1), [128, 4, 8]), op=ALU.mult),
              reads=[b_sm[4], b_sm[0]], writes=[b_sm[5]])
            for s4 in range(4):
                tb, tbb = banks.next()
                for c in range(4):
                    P.op("pe", lambda e, c=c, s4=s4, tb=tb: e.transpose(
                        out=tb[:, c * 128:(c + 1) * 128], in_=xsT[:, c, s4 * 128:(s4 + 1) * 128], identity=idf[:]),
                        reads=[b_xsT[c], b_const], writes=[tbb])
                P.op("act", lambda e, s4=s4, tb=tb: e.activation(out=xtok[:, s4, :], in_=tb, func=AF.Copy),
                     reads=[tbb], writes=[b_xtok[s4]])
                tb2, tbb2 = banks.next()
                tb2b = tb2.bitcast(BF16)
                for g in range(2):
                    P.op("pe", lambda e, g=g, s4=s4, tb2b=tb2b: e.transpose(
                        out=tb2b[:, g * 128:(g + 1) * 128], in_=bcT[:, g, s4 * 128:(s4 + 1) * 128], identity=idb[:]),
                        reads=[b_bcT[g], b_const], writes=[tbb2])
                V(lambda e, s4=s4, tb2b=tb2b: e.tensor_copy(out=btok[:, s4].rearrange("p g n -> p (g n)"), in_=tb2b[:, 0:256]),
                  reads=[tbb2], writes=[b_btok[s4]])
            def do_chunk(s4, sk=sk, own=own, c0=c0):
                par = s4 % 2
                R_, b_R = R_2[par], b_R2[par]
                Abm, b_Abm = Abm2[par], b_Abm2[par]
                eAB, b_eAB = eAB2[par], b_eAB2[par]
                L_, b_L = L_2[par], b_L2[par]
                M_, b_M = M_2[par], b_M2[par]
                CsT, b_CsT = CsT2[par], b_CsT2[par]
                X_, b_X = X_2[par], b_X2[par]
                Xd, b_Xd = Xd2[par], b_Xd2[par]
                ysb, b_ysb = ysb2[par], b_ysb2[par]
                ytmp, b_ytmp = ytmp2[par], b_ytmp2[par]
                szt, b_szt = szt2[par], b_szt2[par]
                junk, b_junk = junk2[par], b_junk2[par]
                gn, b_gn = gn2[par], b_gn2[par]
                cc = sk * 4 + s4
                pbank = [(banks.aps[4 * par + i], banks.bufs[4 * par + i]) for i in range(4)]
                cols = slice(s4 * 128, (s4 + 1) * 128)
                dAc = dA[:, s4, :]
                pa, pab = pbank[0][0][:, 256:512], pbank[0][1]
                P.op("pe", lambda e, pa=pa, dAc=dAc: e.matmul(pa[:, 0:8], lhsT=U[:], rhs=dAc, start=True, stop=True),
                     reads=[b_const, b_sm[5]], writes=[pab])
                V(lambda e, dAc=dAc: e.tensor_tensor(out=R_[:], in0=_bc(U[:].unsqueeze(1), [128, 8, 128]),
                                                      in1=_bc(dAc.unsqueeze(2), [128, 8, 128]), op=ALU.mult),
                  reads=[b_const, b_sm[5]], writes=[b_R])
                pb = [pbank[1], pbank[2]]
                for hh in range(2):
                    P.op("pe", lambda e, hh=hh, pb=pb: e.matmul(
                        pb[hh][0], lhsT=ones[:], rhs=R_[:, 4 * hh:4 * hh + 4, :].rearrange("p a b -> p (a b)"),
                        start=True, stop=True), reads=[b_const, b_R], writes=[pb[hh][1]])
                nA = sm[:, 6 + 6 * par, 0:8]
                V(lambda e, pa=pa, nA=nA: e.tensor_scalar(out=nA, in0=pa[:, 0:8], scalar1=-1.0, scalar2=None, op0=ALU.mult),
                  reads=[pab], writes=[b_sm[6 + 6 * par]])
                tot = sm[:, 7 + 6 * par, 0:8]
                for hh in range(2):
                    V(lambda e, hh=hh, pb=pb, tot=tot: e.tensor_copy(
                        out=tot[:, 4 * hh:4 * hh + 4],
                        in_=pb[hh][0].rearrange("p (a b) -> p a b", a=4)[:, :, 127]),
                      reads=[pb[hh][1]], writes=[b_sm[7 + 6 * par]])
                eT = sm[:, 8 + 6 * par, 0:8]
                P.op("act", lambda e, tot=tot, eT=eT: e.activation(out=eT, in_=tot, func=AF.Exp), reads=[b_sm[7 + 6 * par]], writes=[b_sm[8 + 6 * par]])
                dec = sm[:, 9 + 6 * par, 0:8]
                V(lambda e, tot=tot, nA=nA, dec=dec: e.tensor_tensor(out=dec, in0=tot, in1=nA, op=ALU.add),
                  reads=[b_sm[7 + 6 * par], b_sm[6 + 6 * par]], writes=[b_sm[9 + 6 * par]])
                P.op("act", lambda e, dec=dec: e.activation(out=dec, in_=dec, func=AF.Exp), reads=[b_sm[9 + 6 * par]], writes=[b_sm[9 + 6 * par]])
                scd = sm[:, 10 + 6 * par, 0:8]
                V(lambda e, dec=dec, scd=scd, s4=s4: e.tensor_tensor(out=scd, in0=dec, in1=dtt[:, s4, :], op=ALU.mult),
                  reads=[b_sm[9 + 6 * par], b_sm[4]], writes=[b_sm[10 + 6 * par]])
                xt3 = xtok[:, s4, :].rearrange("p (h d) -> p h d", h=8)
                V(lambda e, xt3=xt3, scd=scd: e.tensor_tensor(out=Xd[:], in0=xt3, in1=_bc(scd.unsqueeze(2), [128, 8, 64]), op=ALU.mult),
                  reads=[b_xtok[s4], b_sm[10 + 6 * par]], writes=[b_Xd])
                pst, pstb = pbank[3]
                for g in range(2):
                    P.op("pe", lambda e, g=g, pst=pst, s4=s4: e.matmul(
                        pst[:, g * 256:(g + 1) * 256], lhsT=btok[:, s4, g, :],
                        rhs=Xd[:, 4 * g:4 * g + 4, :].rearrange("p a b -> p (a b)"), start=True, stop=True),
                        reads=[b_btok[s4], b_Xd], writes=[pstb])
                if own:
                    for hh in range(2):
                        V(lambda e, hh=hh, pb=pb: e.tensor_tensor(
                            out=Abm[:, 4 * hh:4 * hh + 4, :].rearrange("p a b -> p (a b)"), in0=pb[hh][0],
                            in1=mneg[:].rearrange("p a b -> p (a b)"), op=ALU.add),
                          reads=[pb[hh][1], b_const], writes=[b_Abm])
                        P.op("act", lambda e, hh=hh, pb=pb: e.activation(
                            out=eAB[:, 4 * hh:4 * hh + 4, :].rearrange("p a b -> p (a b)"), in_=pb[hh][0], func=AF.Exp),
                            reads=[pb[hh][1]], writes=[b_eAB])
                    for h in range(8):
                        P.op("act", lambda e, h=h, nA=nA: e.activation(out=L_[:, h, :], in_=Abm[:, h, :], func=AF.Exp,
                                                                      bias=nA[:, h:h + 1], scale=1.0),
                             reads=[b_Abm, b_sm[6 + 6 * par]], writes=[b_L])
                    pcb, pcbb = pbank[0]
                    for g in range(2):
                        P.op("pe", lambda e, g=g, pcb=pcb, cols=cols: e.matmul(
                            pcb[:, g * 128:(g + 1) * 128], lhsT=bcT[:, g, cols], rhs=bcT[:, 2 + g, cols],
                            start=True, stop=True), reads=[b_bcT[g], b_bcT[2 + g]], writes=[pcbb])
                    for g in range(2):
                        V(lambda e, g=g, pcb=pcb: e.tensor_tensor(
                            out=M_[:, 4 * g:4 * g + 4, :], in0=_bc(pcb[:, g * 128:(g + 1) * 128].unsqueeze(1), [128, 4, 128]),
                            in1=L_[:, 4 * g:4 * g + 4, :], op=ALU.mult), reads=[pcbb, b_L], writes=[b_M])
                        V(lambda e, g=g, cols=cols: e.tensor_tensor(
                            out=CsT[:, 4 * g:4 * g + 4, :], in0=_bc(bcT[:, 2 + g, cols].unsqueeze(1), [128, 4, 128]),
                            in1=eAB[:, 4 * g:4 * g + 4, :], op=ALU.mult), reads=[b_bcT[2 + g], b_eAB], writes=[b_CsT])
                    V(lambda e, xt3=xt3, s4=s4: e.tensor_tensor(out=X_[:], in0=xt3, in1=_bc(dtt[:, s4, :].unsqueeze(2), [128, 8, 64]), op=ALU.mult),
                      reads=[b_xtok[s4], b_sm[4]], writes=[b_X])
                    P.mark()
                    py, pyb = pbank[1]
                    for h in range(8):
                        P.op("pe", lambda e, h=h, py=py: e.matmul(py[:, h * 64:(h + 1) * 64], lhsT=M_[:, h, :], rhs=X_[:, h, :],
                                                                 start=True, stop=False), reads=[b_M, b_X], writes=[pyb])
                        P.op("pe", lambda e, h=h, py=py: e.matmul(py[:, h * 64:(h + 1) * 64], lhsT=CsT[:, h, :], rhs=Sbf[:, h, :],
                                                                 start=False, stop=True), reads=[b_CsT, b_Sbf], writes=[pyb])
                    P.mark()
                    V(lambda e, s4=s4: e.tensor_tensor(out=ytmp[:], in0=xtok[:, s4, :], in1=dtile[:].rearrange("p a b -> p (a b)"), op=ALU.mult),
                      reads=[b_xtok[s4], b_const], writes=[b_ytmp])
                    V(lambda e, py=py: e.tensor_tensor(out=ysb[:], in0=py, in1=ytmp[:], op=ALU.add),
                      reads=[pyb, b_ytmp], writes=[b_ysb])
                    P.op("act", lambda e, s4=s4: e.activation(out=szt[:], in_=zt[:, s4, :], func=AF.Silu), reads=[b_zt], writes=[b_szt])
                    V(lambda e: e.tensor_tensor(out=ysb[:], in0=ysb[:], in1=szt[:], op=ALU.mult), reads=[b_ysb, b_szt], writes=[b_ysb])
                    ss = sm[:, 11 + 6 * par, 0:2]
                    for g in range(2):
                        P.op("act", lambda e, g=g, ss=ss: e.activation(out=junk[:], in_=ysb[:, g * 256:(g + 1) * 256], func=AF.Square,
                                                                      accum_out=ss[:, g:g + 1]), reads=[b_ysb], writes=[b_junk, b_sm[11 + 6 * par]])
                    P.op("act", lambda e, ss=ss: e.activation(out=ss, in_=ss, func=AF.Sqrt, bias=cst[:, 0:1], scale=1.0 / 256),
                         reads=[b_sm[11 + 6 * par], b_const], writes=[b_sm[11 + 6 * par]])
                    V(lambda e, ss=ss: e.reciprocal(out=ss, in_=ss), reads=[b_sm[11 + 6 * par]], writes=[b_sm[11 + 6 * par]])
                    for g in range(2):
                        V(lambda e, g=g, ss=ss: e.scalar_tensor_tensor(
                            out=gn[:, g * 256:(g + 1) * 256], in0=ysb[:, g * 256:(g + 1) * 256], scalar=ss[:, g:g + 1],
                            in1=nwt[:, g * 256:(g + 1) * 256], op0=ALU.mult, op1=ALU.mult),
                          reads=[b_ysb, b_sm[11 + 6 * par], b_const], writes=[b_gn])
                    pt, ptb = pbank[2]
                    ptb16 = pt.bitcast(BF16)
                    for c in range(4):
                        P.op("pe", lambda e, c=c, ptb16=ptb16: e.transpose(out=ptb16[:, c * 128:(c + 1) * 128], in_=gn[:, c * 128:(c + 1) * 128],
                                                                          identity=idb[:]), reads=[b_gn, b_const], writes=[ptb])
                    P.op("act", lambda e, ptb16=ptb16, cols=cols: e.activation(
                        out=ymT[:, :, cols], in_=ptb16[:, 0:512].rearrange("p (a b) -> p a b", a=4), func=AF.Copy),
                        reads=[ptb], writes=[b_ymT])
                P.mark()
                V(lambda e, eT=eT: e.tensor_tensor(out=Stmp[:], in0=St[:], in1=_bc(eT.unsqueeze(2), [128, 8, 64]), op=ALU.mult),
                  reads=[b_St, b_sm[8 + 6 * par]], writes=[b_Stmp])
                V(lambda e, pst=pst: e.tensor_tensor(out=St[:].rearrange("p a b -> p (a b)"), in0=pst, in1=Stmp[:].rearrange("p a b -> p (a b)"), op=ALU.add),
                  reads=[pstb, b_Stmp], writes=[b_St])
                if NP > 0 and cc == NP // 128 - 1:
                    V(lambda e: e.tensor_scalar(out=St[:], in0=St[:], scalar1=flag[:, 0:1], scalar2=None, op0=ALU.mult),
                      reads=[b_St, b_flag], writes=[b_St])
                P.op("act", lambda e: e.activation(out=Sbf[:], in_=St[:], func=AF.Copy), reads=[b_St], writes=[b_Sbf])
            for pr in range(2):
                caps = []
                for s4 in (2 * pr, 2 * pr + 1):
                    P.begin_capture()
                    do_chunk(s4)
                    parts = P.end_capture()
                    if len(parts) == 2:
                        parts = [parts[0], [], [], parts[1]]
                    assert len(parts) == 4, len(parts)
                    caps.append(parts)
                A_, B_ = caps

                def zipplay(x, y):
                    for i in range(max(len(x), len(y))):
                        if i < len(x):
                            P.replay(x[i])
                        if i < len(y):
                            P.replay(y[i])
                zipplay(A_[0], B_[0])
                for it in A_[1] + A_[3] + B_[1] + B_[3]:
                    P.replay(it)
                zipplay(A_[2], B_[2])
            if own:
                t0 = c0 - NP
                P.dma("sp", B.dr["mixT"][512:1024, t0:t0 + 512].rearrange("(c p) t -> p c t", p=128), ymT[:],
                      reads=[b_ymT], writes=[B.DB("mixT")])
        P.emit()


def attn_stage(B, j, layer):
    nc, P, NT, NP, NK = B.nc, B.P, B.NT, B.NP, B.NK
    lam_init = lam_init_of(layer)
    thr = t5_thresholds()
    NKB = NK // 128
    with ExitStack() as es:
        S = Stage(B, es)
        sb = S.sb
        sbanks = S.banks(4)
        abanks = Ring([es.enter_context(nc.psum_tensor(U_("ab"), [128, 512], F32))[:] for i in range(4)])
        KT = sb("KT", [128, NK], BF16); b_KT = Buf("KT")
        QT = sb("QT", [128, 2, NT], BF16); b_QT = Buf("QT")
        VA = sb("VA", [128, NKB, 136], BF16); b_VA = Buf("VA")
        bv = sb("bv", [128, BV_LAYER], F32); rb = sb("rb", [128, 32, 4], F32); b_bv = Buf("bv")
        flag = sb("flag", [128, 1], F32); b_flag = Buf("flag")
        dist = sb("dist", [128, 2, 128], F32); disti = sb("disti", [128, 2, 128], mybir.dt.int32)
        btile = sb("btile", [128, 4, 2, 256], F32); b_bt = Buf("bt")
        dl = sb("dl", [128, 32, 4], F32)
        tmpb = sb("tmpb", [128, 128], F32); b_tmpb = Buf("tmpb")
        sm = sb("sm", [128, 16, 8], F32); b_sm = [Buf("s") for _ in range(16)]
        subl = sb("subl", [128, 128], F32)
        idb = sb("idb", [128, 128], BF16); idf = sb("idf", [128, 128], F32)
        cst = sb("cst", [128, 2], F32)
        b_const = Buf("const")
        Er = sb_ring(nc, es, "Er", 3, [128, 512], BF16)
        Tr = sb_ring(nc, es, "Tr", 2, [128, 512], F32)
        cbt = sb("cbt", [128, 4, 3, 512], F32)
        o1 = sb("o1", [128, 128], F32); b_o1 = Buf("o1")
        oo = sb("oo", [128, 128], F32); b_oo = Buf("oo")
        junk = sb("junk", [128, 128], F32); b_junk = Buf("junk")
        on = sb("on", [128, 128], BF16); b_on = Buf("on")
        oT = sb("oT", [128, NT], BF16); b_oT = Buf("oT")
        V = lambda f, **kw: P.op("dve", f, **kw)
        A = lambda f, **kw: P.op("act", f, **kw)

        o = j * BV_LAYER
        P.dma("sp", bv[:], B.dr["bvec"][0:1, o:o + BV_LAYER].partition_broadcast(128), writes=[b_bv])
        P.dma("sp", rb[:].rearrange("p a b -> p (a b)"), B.dr["bvec"][0:1, BV_RB:BV_RB + 128].partition_broadcast(128), writes=[b_bv])
        P.dma("sp", flag[:], B.dr["flag"], writes=[b_flag])
        V(lambda e: e.memset(idf[:], 1.0), writes=[b_const])
        V(lambda e: e.memset(cst[:, 0:1], 1e-5), writes=[b_const])
        P.op("pool", lambda e: e.affine_select(out=idf[:], in_=idf[:], pattern=[[1, 128]], compare_op=ALU.is_equal,
                                               fill=0.0, base=0, channel_multiplier=-1), reads=[b_const], writes=[b_const])
        V(lambda e: e.tensor_copy(out=idb[:], in_=idf[:]), reads=[b_const], writes=[b_const])
        V(lambda e: e.memset(QT[:], 0.0), writes=[b_QT])
        V(lambda e: e.memset(VA[:, :, 128:129], 1.0), writes=[b_VA])
        if NP > 0:
            V(lambda e: e.tensor_copy(out=VA[:, 0:NP // 128, 128:129], in_=_bc(flag[:, 0:1].unsqueeze(1), [128, NP // 128, 1])),
              reads=[b_flag], writes=[b_VA])
        for i in range(2):
            V(lambda e, i=i: e.tensor_tensor(out=tmpb[:, 0:64], in0=bv[:, 128 * i:128 * i + 64], in1=bv[:, 128 * i + 64:128 * i + 128], op=ALU.mult),
              reads=[b_bv], writes=[b_tmpb])
            V(lambda e, i=i: e.tensor_reduce(out=sm[:, 0, i:i + 1], in_=tmpb[:, 0:64], axis=AX.X, op=ALU.add),
              reads=[b_tmpb], writes=[b_sm[0]])
        A(lambda e: e.activation(out=sm[:, 0, 0:2], in_=sm[:, 0, 0:2], func=AF.Exp), reads=[b_sm[0]], writes=[b_sm[0]])
        neglam = sm[:, 1, 0:1]
        V(lambda e: e.tensor_tensor(out=neglam, in0=sm[:, 0, 1:2], in1=sm[:, 0, 0:1], op=ALU.subtract), reads=[b_sm[0]], writes=[b_sm[1]])
        V(lambda e: e.tensor_scalar(out=neglam, in0=neglam, scalar1=-lam_init, scalar2=None, op0=ALU.add), reads=[b_sm[1]], writes=[b_sm[1]])
        V(lambda e: e.tensor_scalar(out=subl[:], in0=bv[:, 256:384], scalar1=1.0 - lam_init, scalar2=None, op0=ALU.mult),
          reads=[b_bv], writes=[b_const])
        for d in range(2):
            P.op("pool", lambda e, d=d: e.iota(disti[:, d, :], pattern=[[1, 128]], base=128 * d, channel_multiplier=-1),
                 writes=[b_const])
        V(lambda e: e.tensor_copy(out=dist[:], in_=disti[:]), reads=[b_const], writes=[b_const])
        V(lambda e: e.tensor_tensor(out=dl[:, 1:32, :], in0=rb[:, 1:32, :], in1=rb[:, 0:31, :], op=ALU.subtract), reads=[b_bv], writes=[b_const])
        for h in range(4):
            for d in range(2):
                bt = btile[:, h, d, 0:128]
                V(lambda e, bt=bt, h=h, d=d: e.tensor_scalar(out=bt, in0=dist[:, d, :], scalar1=0.0, scalar2=rb[:, 0, h:h + 1],
                                                            op0=ALU.mult, op1=ALU.add), reads=[b_const, b_bv], writes=[b_bt])
                for b in range(1, 32):
                    if d == 1 and thr[b] <= 1:
                        V(lambda e, bt=bt, h=h, b=b: e.tensor_scalar(out=bt, in0=bt, scalar1=dl[:, b, h:h + 1], scalar2=None, op0=ALU.add),
                          reads=[b_bt, b_const], writes=[b_bt])
                        continue
                    V(lambda e, h=h, d=d, b=b: e.tensor_scalar(out=tmpb[:], in0=dist[:, d, :], scalar1=float(thr[b]) - 0.5,
                                                               scalar2=dl[:, b, h:h + 1], op0=ALU.is_ge, op1=ALU.mult),
                      reads=[b_const], writes=[b_tmpb])
                    V(lambda e, bt=bt: e.tensor_tensor(out=bt, in0=bt, in1=tmpb[:], op=ALU.add), reads=[b_bt, b_tmpb], writes=[b_bt])
                if d == 0:
                    V(lambda e: e.tensor_scalar(out=tmpb[:], in0=dist[:, 0, :], scalar1=-0.5, scalar2=-30000.0,
                                                op0=ALU.is_lt, op1=ALU.mult), reads=[b_const], writes=[b_tmpb])
                    V(lambda e, bt=bt: e.tensor_tensor(out=bt, in0=bt, in1=tmpb[:], op=ALU.add), reads=[b_bt, b_tmpb], writes=[b_bt])
                V(lambda e, bt=bt, h=h, d=d: e.tensor_copy(out=btile[:, h, d, 128:256], in_=bt), reads=[b_bt], writes=[b_bt])

        for h in range(4):
            for half in range(2):
                o_ = half * 256
                V(lambda e, h=h, o_=o_: e.tensor_copy(out=cbt[:, h, 0, o_:o_ + 128], in_=btile[:, h, 1, 0:128]), reads=[b_bt], writes=[b_bt])
                V(lambda e, h=h, o_=o_: e.tensor_scalar(out=cbt[:, h, 0, o_ + 128:o_ + 256], in0=dist[:, 0, :], scalar1=0.0,
                                                       scalar2=rb[:, 31, h:h + 1], op0=ALU.mult, op1=ALU.add),
                  reads=[b_const, b_bv], writes=[b_bt])
                V(lambda e, h=h, o_=o_: e.tensor_copy(out=cbt[:, h, 1, o_:o_ + 128], in_=btile[:, h, 0, 0:128]), reads=[b_bt], writes=[b_bt])
                V(lambda e, h=h, o_=o_: e.tensor_copy(out=cbt[:, h, 1, o_ + 128:o_ + 256], in_=btile[:, h, 1, 0:128]), reads=[b_bt], writes=[b_bt])
                V(lambda e, h=h, o_=o_: e.memset(cbt[:, h, 2, o_:o_ + 128], -30000.0), writes=[b_bt])
                V(lambda e, h=h, o_=o_: e.tensor_copy(out=cbt[:, h, 2, o_ + 128:o_ + 256], in_=btile[:, h, 0, 0:128]), reads=[b_bt], writes=[b_bt])
        vd = B.dr["v"].rearrange("(kb p) f -> p kb f", p=128)
        qb0 = NP // 128
        import os
        DBG = int(os.environ.get("ATT_DEBUG", "9"))
        for h in range(4 if DBG >= 1 else 0):
            P.dma("sp", KT[:], B.dr["kT"][h * 128:(h + 1) * 128, :], reads=[B.DB("kT")], writes=[b_KT])
            P.dma("sp", QT[0:64, 0, :], B.dr["qT"][h * 128:h * 128 + 64, :], reads=[B.DB("qT")], writes=[b_QT])
            P.dma("sp", QT[64:128, 1, :], B.dr["qT"][h * 128 + 64:(h + 1) * 128, :], reads=[B.DB("qT")], writes=[b_QT])
            for k0 in range(0, NKB, 16):
                k1 = min(NKB, k0 + 16)
                P.dma("sp", VA[:, k0:k1, 0:128], vd[:, k0:k1, h * 128:(h + 1) * 128], reads=[B.DB("v")], writes=[b_VA])
            b31 = rb[:, 31, h:h + 1]
            LA = 2
            npair = NT // 256
            tasks = [(jp, kb) for jp in range(npair) for kb in range(qb0 + 2 * jp + 2)]
            acc, Es, deferred = {}, {}, []

            def flush(now, force=False):
                while deferred and (force or now - deferred[0][0] >= 2):
                    deferred.pop(0)[1]()

            def emit_S(i):
                jp, kb = tasks[i]
                qbA = qb0 + 2 * jp
                qc = slice(jp * 256, (jp + 1) * 256)
                kc = slice(kb * 128, (kb + 1) * 128)
                if kb == 0:
                    acc[jp] = [abanks.next() for _ in range(4)]
                sp_, spb = sbanks.next()
                P.op("pe", lambda e: e.matmul(sp_[:, 0:256], lhsT=KT[:, kc], rhs=QT[:, 0, qc], start=True, stop=True),
                     reads=[b_KT, b_QT], writes=[spb])
                P.op("pe", lambda e: e.matmul(sp_[:, 256:512], lhsT=KT[:, kc], rhs=QT[:, 1, qc], start=True, stop=True),
                     reads=[b_KT, b_QT], writes=[spb])
                E, Eb = Er.next()
                dd = qbA - kb
                hh_, b31_ = h, b31
                if dd >= 2:
                    A(lambda e: e.activation(out=E, in_=sp_, func=AF.Exp, bias=b31_, scale=0.125),
                      reads=[spb, b_bv], writes=[Eb])
                else:
                    case = 1 - dd
                    T, Tb = Tr.next()
                    V(lambda e: e.scalar_tensor_tensor(out=T, in0=sp_, scalar=0.125, in1=cbt[:, hh_, case, :],
                                                       op0=ALU.mult, op1=ALU.add), reads=[spb, b_bt], writes=[Tb])
                    A(lambda e: e.activation(out=E, in_=T, func=AF.Exp), reads=[Tb], writes=[Eb])
                Es[i] = (E, Eb)

            def emit_PV(i, now):
                jp, kb = tasks[i]
                qbA = qb0 + 2 * jp
                qbB = qbA + 1
                (a1, a1b), (a2, a2b), (c1, c1b), (c2, c2b) = acc[jp]
                E, Eb = Es.pop(i)
                if kb <= qbA:
                    P.op("pe", lambda e: e.matmul(a1[:, 0:129], lhsT=E[:, 0:128], rhs=VA[:, kb, 0:129],
                                                  start=(kb == 0), stop=(kb == qbA)), reads=[Eb, b_VA], writes=[a1b])
                    P.op("pe", lambda e: e.matmul(a2[:, 0:129], lhsT=E[:, 256:384], rhs=VA[:, kb, 0:129],
                                                  start=(kb == 0), stop=(kb == qbA)), reads=[Eb, b_VA], writes=[a2b])
                P.op("pe", lambda e: e.matmul(c1[:, 0:129], lhsT=E[:, 128:256], rhs=VA[:, kb, 0:129],
                                              start=(kb == 0), stop=(kb == qbB)), reads=[Eb, b_VA], writes=[c1b])
                P.op("pe", lambda e: e.matmul(c2[:, 0:129], lhsT=E[:, 384:512], rhs=VA[:, kb, 0:129],
                                              start=(kb == 0), stop=(kb == qbB)), reads=[Eb, b_VA], writes=[c2b])
                if kb == qbA:
                    epilogue(2 * jp, acc[jp][0], acc[jp][1], now)
                if kb == qbB:
                    epilogue(2 * jp + 1, acc[jp][2], acc[jp][3], now)
                    acc.pop(jp)

            def epilogue(jq, A1, A2, now):
                flush(now, force=True)
                a1, a1b = A1
                a2, a2b = A2
                qc = slice(jq * 128, (jq + 1) * 128)
                r = sm[:, 2 + (jq % 2) * 2, 0:4]
                rbuf = b_sm[2 + (jq % 2) * 2]
                V(lambda e: e.reciprocal(out=r[:, 0:1], in_=a1[:, 128:129]), reads=[a1b], writes=[rbuf])
                V(lambda e: e.reciprocal(out=r[:, 1:2], in_=a2[:, 128:129]), reads=[a2b], writes=[rbuf])
                V(lambda e: e.tensor_tensor(out=r[:, 1:2], in0=r[:, 1:2], in1=neglam, op=ALU.mult), reads=[rbuf, b_sm[1]], writes=[rbuf])
                V(lambda e: e.tensor_scalar(out=o1[:], in0=a1[:, 0:128], scalar1=r[:, 0:1], scalar2=None, op0=ALU.mult),
                  reads=[a1b, rbuf], writes=[b_o1])
                V(lambda e: e.scalar_tensor_tensor(out=oo[:], in0=a2[:, 0:128], scalar=r[:, 1:2], in1=o1[:], op0=ALU.mult, op1=ALU.add),
                  reads=[a2b, rbuf, b_o1], writes=[b_oo])
                A(lambda e: e.activation(out=junk[:], in_=oo[:], func=AF.Square, accum_out=r[:, 2:3]), reads=[b_oo], writes=[b_junk, rbuf])
                A(lambda e: e.activation(out=r[:, 2:3], in_=r[:, 2:3], func=AF.Sqrt, bias=cst[:, 0:1], scale=1.0 / 128),
                  reads=[rbuf, b_const], writes=[rbuf])
                V(lambda e: e.reciprocal(out=r[:, 2:3], in_=r[:, 2:3]), reads=[rbuf], writes=[rbuf])
                V(lambda e: e.scalar_tensor_tensor(out=on[:], in0=oo[:], scalar=r[:, 2:3], in1=subl[:], op0=ALU.mult, op1=ALU.mult),
                  reads=[b_oo, rbuf, b_const], writes=[b_on])

                def fin():
                    tp, tpb = sbanks.next()
                    tp16 = tp.bitcast(BF16)
                    P.op("pe", lambda e: e.transpose(out=tp16[:, 0:128], in_=on[:], identity=idb[:]), reads=[b_on, b_const], writes=[tpb])
                    V(lambda e: e.tensor_copy(out=oT[:, qc], in_=tp16[:, 0:128]), reads=[tpb], writes=[b_oT])
                deferred.append((now, fin))

            nt_ = len(tasks) if DBG >= 2 else 0
            for i in range(nt_ + LA if nt_ else 0):
                if i < nt_:
                    emit_S(i)
                if i >= LA:
                    emit_PV(i - LA, i)
                flush(i)
            flush(0, force=True)
            P.dma("sp", B.dr["mixT"][h * 128:(h + 1) * 128, :], oT[:], reads=[b_oT], writes=[B.DB("mixT")])
        P.emit()


def build_full(NT):
    B = Builder(NT, 0, ins=["xT_i", "flag"], outs=["outT"])
    declare_common(B, list(WSHAPES.keys()))
    B.D("xT_i", [1024, NT], F32)
    B.D("flag", [128, 1], F32)
    B.D("outT", [1024, NT], F32)
    B.D("xT", [1024, NT], F32)
    B.D("qT", [512, NT], BF16)
    B.D("kT", [512, NT], BF16)
    B.D("v", [NT, 512], BF16)
    B.D("z", [NT, 512], F32)
    B.D("xbcT", [1024, NT], F32)
    B.D("dt", [NT, 8], F32)
    B.D("mixT", [1024, NT], BF16)
    B.D("vvT", [1024, NT + 2], F32)
    B.D("bgT", [1024, NT], F32)
    def cast_ffn(fi):
        for pi in range(2):
            for g in range(2):
                B.cast("wgu", (fi, g), 2, only=pi)
        B.cast("wd", (fi,), 2)
    cast_ffn(0)
    B.cast("hwin_f", (0,), 2)
    B.cast("hwin_t", (0,), 1)
    B.cast("hwout", (0,), 1)
    cast_ffn(1)
    cast_ffn(2)
    B.cast("scwin", (0,), 3)
    B.cast("scwout", (0,), 1)
    cast_ffn(3)
    cast_ffn(4)
    B.cast("hwin_f", (1,), 2)
    B.cast("hwin_t", (1,), 1)
    B.cast("hwout", (1,), 1)
    cast_ffn(5)
    cast_ffn(6)
    B.cast("scwin", (1,), 3)
    B.cast("scwout", (1,), 1)
    cast_ffn(7)
    import os
    nst = int(os.environ.get("NSTAGES", "99"))
    stages = [
        lambda: B.tl_segment("xT_i", "xT", None, [(0, 0)], ("even", 0, 1, 0)),
        lambda: ssd_stage(B, 0),
        lambda: attn_stage(B, 0, 0),
        lambda: B.tl_segment("xT", "xT", ("even", 0), [(1, 2), (2, 3)], ("odd", 0, 4)),
        lambda: B.tl_segment("xT", "xT", ("odd", 0), [(3, 5), (4, 6)], ("even", 1, 7, 0)),
        lambda: ssd_stage(B, 1),
        lambda: attn_stage(B, 1, 2),
        lambda: B.tl_segment("xT", "xT", ("even", 1), [(5, 8), (6, 9)], ("odd", 1, 10)),
        lambda: B.tl_segment("xT", None, ("odd", 1), [(7, 11)], ("final", 12)),
    ]
    for st in stages[:nst]:
        st()
    return B


def kernel(**inputs):
    x = np.asarray(inputs["x"], dtype=np.float32)
    Bsz, S, Dm = x.shape
    W = host_weights(inputs)
    Bd = build_full(S)
    flag = np.ones((128, 1), np.float32)
    xts = [np.ascontiguousarray(x[b].T) for b in range(Bsz)]
    owners = [0, 1, 4, 5]
    zW = {n: np.zeros_like(W[n]) for n in W}
    zx = np.zeros_like(xts[0])
    in_maps = []
    for c in range(8):
        if c in owners:
            m = {"xT_i": xts[owners.index(c)], "flag": flag}
            m.update(W)
        else:
            m = {"xT_i": zx, "flag": flag}
            m.update(zW)
        in_maps.append(m)
    res = run_bass_kernel_spmd(Bd.nc, in_maps, core_ids=list(range(8)))
    out = np.stack([np.ascontiguousarray(res.results[c]["outT"].T) for c in owners])
    return out.astype(np.float32)
```

```python
import math
from contextlib import ExitStack

import numpy as np
import concourse.bass as bass
import concourse.mybir as mybir
from concourse.bass_utils import run_bass_kernel_spmd

F32 = mybir.dt.float32
BF16 = mybir.dt.bfloat16
AF = mybir.ActivationFunctionType
ALU = mybir.AluOpType
AX = mybir.AxisListType

D_MODEL = 1024
BATCH = 4
SEQ = 8192
DEPTH = 4
D_FF = 2816
NFC = D_FF // 128
NDC = D_MODEL // 128
HYB_IN = 3080
TT = 512


class Buf:
    __slots__ = ("name", "lastw", "rd_c", "rd_d")

    def __init__(self, name):
        self.name = name
        self.lastw = None
        self.rd_c = {}
        self.rd_d = []


class Prog:
    ENGS = ("pe", "act", "dve", "pool", "sp")
    NDMASEM = 24

    def __init__(self, nc, es):
        self.nc = nc
        self.stage = 0
        self.ops = {e: [] for e in self.ENGS}
        self.ndma = {e: 0 for e in self.ENGS}
        self.dma_tok = {e: [] for e in self.ENGS}
        self.base = {e: 0 for e in self.ENGS}
        self.csem = {e: es.enter_context(nc.semaphore("cs_" + e)) for e in self.ENGS}
        self.dsem = {e: [es.enter_context(nc.semaphore("ds_%s_%d" % (e, i)))
                         for i in range(self.NDMASEM)] for e in ("sp", "pool")}
        self.eng = {"pe": nc.tensor, "act": nc.scalar, "dve": nc.vector,
                    "pool": nc.gpsimd, "sp": nc.sync}

    def _deps(self, reads, writes):
        cd = {}
        dd = []

        def add(tok):
            if tok is None or tok[1] != self.stage:
                return
            if tok[0] == "c":
                if cd.get(tok[2], -1) < tok[3]:
                    cd[tok[2]] = tok[3]
            else:
                dd.append(tok)
        for b in reads:
            add(b.lastw)
        for b in writes:
            add(b.lastw)
            for e, (st, i) in b.rd_c.items():
                add(("c", st, e, i))
            for t in b.rd_d:
                add(t)
        return cd, dd

    def _commit(self, tok, reads, writes):
        for b in reads:
            if tok[0] == "c":
                b.rd_c[tok[2]] = (tok[1], tok[3])
            else:
                b.rd_d = [t for t in b.rd_d if t[1] == self.stage][-8:] + [tok]
        for b in writes:
            b.lastw = tok
            b.rd_c = {}
            b.rd_d = []

    _cap = None

    def begin_capture(self):
        self._cap = [[]]

    def mark(self):
        if self._cap is not None:
            self._cap.append([])

    def end_capture(self):
        c, self._cap = self._cap, None
        return c

    def replay(self, item):
        if item[0] == "op":
            self.op(*item[1:])
        else:
            self.dma(*item[1:])

    def op(self, eng, fn, reads=(), writes=()):
        if self._cap is not None:
            self._cap[-1].append(("op", eng, fn, list(reads), list(writes)))
            return None
        cd, dd = self._deps(reads, writes)
        idx = len(self.ops[eng])
        tok = ("c", self.stage, eng, idx)
        self.ops[eng].append({"fn": fn, "cd": cd, "dd": dd, "dma": None})
        self._commit(tok, reads, writes)
        return tok

    def dma(self, q, out, in_, reads=(), writes=()):
        if self._cap is not None:
            self._cap[-1].append(("dma", q, out, in_, list(reads), list(writes)))
            return None
        cd, dd = self._deps(reads, writes)
        k = self.ndma[q]
        self.ndma[q] += 1
        slot = k % self.NDMASEM
        val = 16 * (k // self.NDMASEM + 1)
        if k >= self.NDMASEM:
            pt = self.dma_tok[q][k - self.NDMASEM]
            if pt[1] == self.stage:
                dd.append(pt)
        if q == "pool" and k >= 6:
            pt = self.dma_tok[q][k - 6]
            if pt[1] == self.stage:
                dd.append(pt)
        tok = ("d", self.stage, q, slot, val)
        self.dma_tok[q].append(tok)
        self.ops[q].append({"fn": None, "cd": cd, "dd": dd, "dma": (out, in_, slot)})
        self._commit(tok, reads, writes)
        return tok

    def emit(self):
        nc = self.nc
        need = {e: set() for e in self.ENGS}
        for e in self.ENGS:
            for o in self.ops[e]:
                for de, di in o["cd"].items():
                    if de == "pe" and e == "pe":
                        continue
                    need[de].add(di)
        cnt = {}
        for e in self.ENGS:
            c = self.base[e]
            arr = []
            for i in range(len(self.ops[e])):
                if i in need[e]:
                    c += 1
                arr.append(c)
            cnt[e] = arr
        final_waits = []
        for q in ("sp", "pool"):
            n = self.ndma[q]
            for s in range(min(n, self.NDMASEM)):
                last = (n - 1 - s) // self.NDMASEM
                final_waits.append((self.dsem[q][s], 16 * (last + 1)))
        with nc.Block() as block:
            def body(e):
                eng = self.eng[e]
                waited = {}
                for i, o in enumerate(self.ops[e]):
                    for de, di in o["cd"].items():
                        if de == "pe" and e == "pe":
                            continue
                        v = cnt[de][di]
                        if waited.get(de, 0) >= v:
                            continue
                        waited[de] = v
                        eng.wait_ge(self.csem[de], v)
                    for d in o["dd"]:
                        key = (d[2], d[3])
                        if waited.get(key, 0) >= d[4]:
                            continue
                        waited[key] = d[4]
                        eng.wait_ge(self.dsem[d[2]][d[3]], d[4])
                    if o["dma"] is not None:
                        out, in_, slot = o["dma"]
                        eng.dma_start(out=out, in_=in_).then_inc(self.dsem[e][slot], 16)
                    else:
                        inst = o["fn"](eng)
                        if i in need[e]:
                            inst.then_inc(self.csem[e], 1)
                if e == "sp":
                    for sem, v in final_waits:
                        eng.wait_ge(sem, v)

            @block.tensor
            def _(t):
                body("pe")

            @block.scalar
            def _(t):
                body("act")

            @block.vector
            def _(t):
                body("dve")

            @block.gpsimd
            def _(t):
                body("pool")

            @block.sync
            def _(t):
                body("sp")
        for e in self.ENGS:
            if cnt[e]:
                self.base[e] = cnt[e][-1]
            self.ops[e] = []
        self.stage += 1


class Ctx:
    pass


_UID = [0]


def U_(name):
    _UID[0] += 1
    return "%s_%d" % (name, _UID[0])


class Ring:
    def __init__(self, aps):
        self.aps = aps
        self.bufs = [Buf("r") for _ in aps]
        self.i = 0

    def next(self):
        k = self.i % len(self.aps)
        self.i += 1
        return self.aps[k], self.bufs[k]


def sb_ring(nc, es, name, n, shape, dt):
    t = es.enter_context(nc.sbuf_tensor(U_(name), [shape[0], n] + list(shape[1:]), dt))
    return Ring([t[:, i] for i in range(n)])


def run_steps(steps, depth=2):
    issued = 0
    handles = []
    n = len(steps)
    for i in range(n):
        while issued < min(n, i + depth + 1):
            lf = steps[issued][0]
            handles.append(lf() if lf is not None else None)
            issued += 1
        steps[i][1](handles[i])
        handles[i] = None


def t5_thresholds():
    d = np.arange(0, 256)
    max_exact = 16
    with np.errstate(divide="ignore"):
        large = max_exact + (np.log(np.maximum(d, 1).astype(np.float32) / np.float32(max_exact))
                             / np.float32(math.log(128 / max_exact)) * np.float32(32 - max_exact)).astype(np.int32)
    large = np.minimum(large, 31)
    bucket = np.where(d < max_exact, d, large)
    thr = [int(np.min(d[bucket >= b])) for b in range(32)]
    return thr


BV_LAYER = 920
BV_RB = 2 * BV_LAYER
BV_N = BV_RB + 128
PV_NORM = 0
PV_CW = 104
PV_CB = PV_CW + 64
PV_SCW = PV_CB + 16
PV_N = PV_SCW + 48


def lam_init_of(layer):
    return 0.8 - 0.6 * math.exp(-0.3 * layer)


class Builder:
    def __init__(self, NT, NP, ins, outs):
        self.NT, self.NP, self.NK = NT, NP, NT + NP
        self.ins, self.outs = set(ins), set(outs)
        self.nc = bass.Bass("TRN2", target_bir_lowering=False)
        self.es = ExitStack()
        self.P = Prog(self.nc, self.es)
        self.dr = {}
        self.dbuf = {}
        self.castbuf = {}

    def D(self, name, shape=None, dt=None):
        if name not in self.dr:
            kind = ("ExternalInput" if name in self.ins else
                    "ExternalOutput" if name in self.outs else "Internal")
            self.dr[name] = self.nc.dram_tensor(name, list(shape), dt, kind=kind).ap()
            self.dbuf[name] = Buf(name)
        return self.dr[name]

    def DB(self, name):
        return self.dbuf[name]

    def weight(self, name, shape):
        self.ins.add(name)
        src = self.D(name, shape, F32)
        dst = self.D(name + "_bf", shape, BF16)
        return src, dst

    def cast(self, name, idx, npieces=1):
        src, dst = self.dr[name], self.dr[name + "_bf"]
        s, d = src, dst
        for i in idx:
            s, d = s[i], d[i]
        n = s.shape[0]
        step = (n + npieces - 1) // npieces
        for pi, a in enumerate(range(0, n, step)):
            b = Buf("cast")
            self.castbuf[(name, idx, pi)] = (b, a, min(n, a + step))
            self.P.dma("pool", d[a:min(n, a + step)], s[a:min(n, a + step)], writes=[b])

    def castbuf_for(self, name, idx, k):
        pi = 0
        while True:
            b, a, e = self.castbuf[(name, idx, pi)]
            if a <= k < e:
                return b
            pi += 1

    def tl_segment(self, x_src, x_dst, pre, ffns, post, tiles=None):
        nc, P, NT, NP = self.nc, self.P, self.NT, self.NP
        ntile = NT // TT
        import os
        with ExitStack() as es:
            def sb(name, shape, dt):
                return es.enter_context(nc.sbuf_tensor(U_(name), shape, dt))
            NXT = 2
            xts = [sb("xt", [128, 8, TT], F32) for _ in range(NXT)]
            b_xts = [[Buf("xt") for _ in range(8)] for _ in range(NXT)]
            cur = {}
            hT = sb("hT", [128, 8, TT], BF16)
            aT = sb("aT", [128, NFC, TT], BF16)
            ones = sb("ones", [128, 128], F32)
            cst = sb("cst", [128, 4], F32)
            pv = sb("pv", [128, PV_N], F32)
            rs = sb("rs", [128, TT], F32)
            rstd = sb("rstd", [128, TT], F32)
            b_ones, b_cst, b_pv, b_rs, b_rstd = [Buf(n) for n in range(5)]
            b_hT = [Buf("hT") for _ in range(8)]
            b_aT = [Buf("aT") for _ in range(NFC)]
            sq = sb_ring(nc, es, "sq", 2, [128, TT], F32)
            sil = sb_ring(nc, es, "sil", 2, [128, TT], F32)
            gu = sb_ring(nc, es, "gu", 3, [128, 2, 2, 8, 128], BF16)
            wdr = sb_ring(nc, es, "wdr", 3, [128, NFC, 128], BF16)
            w1 = sb_ring(nc, es, "w1", 9 if post[0] == "odd" else 6, [128, 8, 128], BF16)
            s32 = sb_ring(nc, es, "s32", 4, [128, TT], F32)
            s16 = sb_ring(nc, es, "s16", 4, [128, TT], BF16)
            banks = Ring([es.enter_context(nc.psum_tensor(U_("bk"), [128, 512], F32))[:]
                          for i in range(7)])
            nbank = es.enter_context(nc.psum_tensor(U_("nbk"), [128, 512], F32))[:]
            b_nb = Buf("nbank")
            if pre is not None:
                mixs = [sb("mix", [128, 8, TT], BF16) for _ in range(2)]
                b_mixs = [Buf("mix") for _ in range(2)]
            if pre is not None and pre[0] == "odd":
                vvr = sb_ring(nc, es, "vvr", 3, [128, TT + 2], F32)
                bgr = sb_ring(nc, es, "bgr", 3, [128, TT], F32)
                cacc = sb_ring(nc, es, "cacc", 2, [128, TT], F32)
            if post[0] == "even":
                w8k = sb_ring(nc, es, "w8k", 1, [128, 8, 512], BF16)
                wdt = sb("wdt", [128, 8, 8], BF16)
                dts = sb("dts", [128, 4, 8], F32)
                b_wdt, b_dts = Buf("wdt"), Buf("dts")

            P.op("dve", lambda e: e.memset(ones[:], 1.0), writes=[b_ones])
            P.op("dve", lambda e: e.memset(cst[:, 0:1], 1e-6), writes=[b_cst])
            P.op("dve", lambda e: e.memset(cst[:, 1:2], 1e-5), writes=[b_cst])
            P.op("dve", lambda e: e.memset(cst[:, 2:3], 1.0), writes=[b_cst])
            P.op("dve", lambda e: e.memset(cst[:, 3:4], 0.0), writes=[b_cst])
            P.dma("sp", pv[:], self.dr["pvec"], writes=[b_pv])
            if post[0] == "even":
                j = post[1]
                P.dma("sp", wdt[:], self.dr["hwin_t_bf"][j, 0][:, :, 1024:1032],
                      reads=[self.castbuf_for("hwin_t", (j,), 0)], writes=[b_wdt])

            def norm_sq(c):
                xt, b_xt = cur["xt"], cur["bx"]
                s, sbf = sq.next()
                P.op("act", lambda e: e.activation(out=s, in_=xt[:, c, :], func=AF.Square),
                     reads=[b_xt[c]], writes=[sbf])
                return s, sbf

            def norm_mm(c, ssb):
                s, sbf = ssb
                P.op("pe", lambda e: e.matmul(nbank, lhsT=ones[:], rhs=s, start=(c == 0), stop=(c == 7)),
                     reads=[sbf, b_ones], writes=[b_nb])

            def norm_fin(nspec):
                xt, b_xt = cur["xt"], cur["bx"]
                nidx, final, t0 = nspec
                P.op("act", lambda e: e.activation(out=rs[:], in_=nbank, func=AF.Sqrt, bias=cst[:, 0:1], scale=1.0 / 1024),
                     reads=[b_nb, b_cst], writes=[b_rs])
                P.op("dve", lambda e: e.reciprocal(out=rstd[:], in_=rs[:]), reads=[b_rs], writes=[b_rstd])
                for c in range(8):
                    col = PV_NORM + nidx * 8 + c
                    if not final:
                        P.op("dve", lambda e, c=c, col=col: e.scalar_tensor_tensor(
                            out=hT[:, c, :], in0=xt[:, c, :], scalar=pv[:, col:col + 1], in1=rstd[:],
                            op0=ALU.mult, op1=ALU.mult), reads=[b_xt[c], b_rstd, b_pv], writes=[b_hT[c]])
                    else:
                        s, sbf = s32.next()
                        P.op("dve", lambda e, c=c, col=col, s=s: e.scalar_tensor_tensor(
                            out=s, in0=xt[:, c, :], scalar=pv[:, col:col + 1], in1=rstd[:],
                            op0=ALU.mult, op1=ALU.mult), reads=[b_xt[c], b_rstd, b_pv], writes=[sbf])
                        P.dma("sp", self.dr["outT"][c * 128:(c + 1) * 128, t0:t0 + TT], s, reads=[sbf],
                              writes=[self.DB("outT")])

            def norm_full(nspec):
                for c in range(8):
                    norm_mm(c, norm_sq(c))
                norm_fin(nspec)

            class Upd:
                def __init__(self, nspec):
                    self.nspec = nspec
                    self.pend = None

                def after_mm(self):
                    if self.pend is not None:
                        norm_mm(*self.pend)
                        self.pend = None

                def after_update(self, d):
                    if self.nspec is None:
                        return
                    self.pend = (d, norm_sq(d))
                    if d == 7:
                        norm_mm(*self.pend)
                        self.pend = None
                        norm_fin(self.nspec)

            def ffn_steps(fi, nspec):
                steps = []
                wgu_bf, wd_bf = self.dr["wgu_bf"], self.dr["wd_bf"]
                upd = Upd(nspec)
                for f2 in range(0, NFC, 2):
                    def loads(f2=f2):
                        slot, sbf = gu.next()
                        for g in range(2):
                            P.dma("sp", slot[:, g], wgu_bf[fi, g, f2:f2 + 2].rearrange("f p k j -> p f k j"),
                                  reads=[self.castbuf_for("wgu", (fi, g), f2), self.castbuf_for("wgu", (fi, g), f2 + 1)],
                                  writes=[sbf])
                        return slot, sbf

                    def comp(h, f2=f2):
                        slot, sbf = h
                        for ff in range(2):
                            f = f2 + ff
                            pg, bg = banks.next()
                            pu, bu = banks.next()
                            for kc in range(8):
                                P.op("pe", lambda e, kc=kc, ff=ff, pg=pg: e.matmul(pg, lhsT=slot[:, 0, ff, kc, :], rhs=hT[:, kc, :],
                                                                                   start=(kc == 0), stop=(kc == 7)),
                                     reads=[sbf, b_hT[kc]], writes=[bg])
                            for kc in range(8):
                                P.op("pe", lambda e, kc=kc, ff=ff, pu=pu: e.matmul(pu, lhsT=slot[:, 1, ff, kc, :], rhs=hT[:, kc, :],
                                                                                   start=(kc == 0), stop=(kc == 7)),
                                     reads=[sbf, b_hT[kc]], writes=[bu])
                            s_, ssb = sil.next()
                            P.op("act", lambda e, s_=s_, pg=pg: e.activation(out=s_, in_=pg, func=AF.Silu), reads=[bg], writes=[ssb])
                            P.op("dve", lambda e, s_=s_, pu=pu, f=f: e.tensor_tensor(out=aT[:, f, :], in0=pu, in1=s_, op=ALU.mult),
                                 reads=[bu, ssb], writes=[b_aT[f]])
                    steps.append((loads, comp))
                for d in range(8):
                    def loads(d=d):
                        slot, sbf = wdr.next()
                        P.dma("sp", slot, wd_bf[fi, d], reads=[self.castbuf_for("wd", (fi,), d)], writes=[sbf])
                        return slot, sbf

                    def comp(h, d=d):
                        xt, b_xt = cur["xt"], cur["bx"]
                        slot, sbf = h
                        py, by = banks.next()
                        for fc in range(NFC):
                            P.op("pe", lambda e, fc=fc: e.matmul(py, lhsT=slot[:, fc, :], rhs=aT[:, fc, :],
                                                                  start=(fc == 0), stop=(fc == NFC - 1)),
                                 reads=[sbf, b_aT[fc]], writes=[by])
                        upd.after_mm()
                        P.op("dve", lambda e: e.scalar_tensor_tensor(out=xt[:, d, :], in0=py, scalar=0.5, in1=xt[:, d, :],
                                                                     op0=ALU.mult, op1=ALU.add),
                             reads=[by, b_xt[d]], writes=[b_xt[d]])
                        upd.after_update(d)
                    steps.append((loads, comp))
                return steps

            def proj_fm_group(wname, j, items):
                w_bf = self.dr[wname + "_bf"]

                def loads():
                    hs = []
                    for ci, _ in items:
                        slot, sbf = w1.next()
                        P.dma("sp", slot, w_bf[j, ci], reads=[self.castbuf_for(wname, (j,), ci)], writes=[sbf])
                        hs.append((slot, sbf))
                    return hs

                def comp(h):
                    for (slot, sbf), (ci, evac) in zip(h, items):
                        bank, bb = banks.next()
                        for kc in range(8):
                            P.op("pe", lambda e, kc=kc, slot=slot, bank=bank: e.matmul(bank, lhsT=slot[:, kc, :], rhs=hT[:, kc, :],
                                                                                       start=(kc == 0), stop=(kc == 7)),
                                 reads=[sbf, b_hT[kc]], writes=[bb])
                        evac(bank, bb)
                return (loads, comp)

            def outproj_steps(wname, j, nspec):
                steps = []
                w_bf = self.dr[wname + "_bf"]
                upd = Upd(nspec)
                for d2 in range(0, 8, 2):
                    def loads(d2=d2):
                        hs = []
                        for d in (d2, d2 + 1):
                            slot, sbf = w1.next()
                            P.dma("sp", slot, w_bf[j, d], reads=[self.castbuf_for(wname, (j,), d)], writes=[sbf])
                            hs.append((slot, sbf))
                        return hs

                    def comp(h, d2=d2):
                        xt, b_xt = cur["xt"], cur["bx"]
                        mix, b_mix = cur["mix"], cur["bm"]
                        for (slot, sbf), d in zip(h, (d2, d2 + 1)):
                            bank, bb = banks.next()
                            for mc in range(8):
                                P.op("pe", lambda e, mc=mc, slot=slot, bank=bank: e.matmul(bank, lhsT=slot[:, mc, :], rhs=mix[:, mc, :],
                                                                                           start=(mc == 0), stop=(mc == 7)),
                                     reads=[sbf, b_mix], writes=[bb])
                            upd.after_mm()
                            P.op("dve", lambda e, bank=bank, d=d: e.tensor_tensor(out=xt[:, d, :], in0=bank, in1=xt[:, d, :], op=ALU.add),
                                 reads=[bb, b_xt[d]], writes=[b_xt[d]])
                            upd.after_update(d)
                    steps.append((loads, comp))
                return steps

            steps = []
            xs_ap = self.dr[x_src].rearrange("(c p) t -> p c t", p=128)
            tlist = list(range(ntile) if tiles is None else tiles)
            mx = self.dr["mixT"].rearrange("(c p) t -> p c t", p=128) if (pre is not None and pre[0] == "even") else None

            def conv_steps(ti):
                j = pre[1]
                t0_ = tlist[ti] * TT
                mix, b_mix = mixs[ti % 2], b_mixs[ti % 2]
                vv = self.dr["vvT"]
                bgd = self.dr["bgT"]
                out = []
                for c in range(8):
                    def ld_v(c=c):
                        v_, vb = vvr.next()
                        g_, gb = bgr.next()
                        P.dma("sp", v_, vv[c * 128:(c + 1) * 128, t0_:t0_ + TT + 2], reads=[self.DB("vvT")], writes=[vb])
                        P.dma("sp", g_, bgd[c * 128:(c + 1) * 128, t0_:t0_ + TT], reads=[self.DB("bgT")], writes=[gb])
                        return v_, vb, g_, gb

                    def conv(h, c=c):
                        v_, vb, g_, gb = h
                        a, ab = cacc.next()
                        wc = PV_SCW + (j * 8 + c) * 3
                        P.op("dve", lambda e: e.tensor_scalar(
                            out=a, in0=v_[:, 2:TT + 2], scalar1=pv[:, wc + 2:wc + 3], scalar2=None,
                            op0=ALU.mult), reads=[vb, b_pv], writes=[ab])
                        P.op("dve", lambda e: e.scalar_tensor_tensor(
                            out=a, in0=v_[:, 1:TT + 1], scalar=pv[:, wc + 1:wc + 2], in1=a,
                            op0=ALU.mult, op1=ALU.add), reads=[vb, b_pv, ab], writes=[ab])
                        P.op("dve", lambda e: e.scalar_tensor_tensor(
                            out=a, in0=v_[:, 0:TT], scalar=pv[:, wc:wc + 1], in1=a,
                            op0=ALU.mult, op1=ALU.add), reads=[vb, b_pv, ab], writes=[ab])
                        P.op("dve", lambda e: e.tensor_tensor(
                            out=mix[:, c, :], in0=a, in1=g_, op=ALU.mult),
                            reads=[ab, gb], writes=[b_mix])
                    out.append((ld_v, conv))
                return out

            def prefetch(ti):
                if ti >= len(tlist):
                    return
                t0_ = tlist[ti] * TT
                P.dma("sp", xts[ti % NXT][:], xs_ap[:, :, t0_:t0_ + TT], reads=[self.DB(x_src)], writes=b_xts[ti % NXT])
                if mx is not None:
                    P.dma("sp", mixs[ti % 2][:], mx[:, :, t0_:t0_ + TT], reads=[self.DB("mixT")], writes=[b_mixs[ti % 2]])
            steps.append((None, lambda h: prefetch(0)))
            for ti, t in enumerate(tlist):
                t0 = t * TT
                xslot, bxs = xts[ti % NXT], b_xts[ti % NXT]

                def set_cur(h, xslot=xslot, bxs=bxs, ti=ti):
                    cur["xt"], cur["bx"] = xslot, bxs
                    if pre is not None:
                        cur["mix"], cur["bm"] = mixs[ti % 2], b_mixs[ti % 2]
                steps.append((None, set_cur))
                tile_first_step = len(steps)
                nlist = [(nidx, False, t0) for (_, nidx) in ffns]
                if post[0] == "final":
                    nlist.append((post[1], True, t0))
                else:
                    nlist.append((post[2], False, t0))
                if pre is not None and pre[0] == "even":
                    steps += outproj_steps("hwout", pre[1], nlist[0])
                elif pre is not None:
                    if ti == 0:
                        steps += conv_steps(0)
                    steps += outproj_steps("scwout", pre[1], nlist[0])
                else:
                    steps.append((None, lambda h, ns=nlist[0]: norm_full(ns)))
                for k, (fi, nidx) in enumerate(ffns):
                    steps += ffn_steps(fi, nlist[k + 1])
                if post[0] == "final":
                    pass
                elif post[0] == "odd":
                    j = post[1]
                    for c in range(8):
                        hold = {}

                        def ev_cg(bank, bb, hold=hold):
                            s, sbf = s32.next()
                            hold["cg"] = (s, sbf)
                            P.op("act", lambda e: e.activation(out=s, in_=bank, func=AF.Copy), reads=[bb], writes=[sbf])

                        def ev_u(bank, bb, hold=hold, c=c, t0=t0):
                            s, sbf = hold["cg"]
                            o, obf = s32.next()
                            P.op("dve", lambda e: e.tensor_tensor(out=o, in0=bank, in1=s, op=ALU.mult),
                                 reads=[bb, sbf], writes=[obf])
                            P.dma("sp", self.dr["vvT"][c * 128:(c + 1) * 128, 2 + t0:2 + t0 + TT], o,
                                  reads=[obf], writes=[self.DB("vvT")])

                        def ev_bg(bank, bb, c=c, t0=t0):
                            s, sbf = s32.next()
                            P.op("act", lambda e: e.activation(out=s, in_=bank, func=AF.Copy), reads=[bb], writes=[sbf])
                            P.dma("sp", self.dr["bgT"][c * 128:(c + 1) * 128, t0:t0 + TT], s,
                                  reads=[sbf], writes=[self.DB("bgT")])
                        steps.append(proj_fm_group("scwin", j, [(8 + c, ev_cg), (16 + c, ev_u), (c, ev_bg)]))
                else:
                    j, NPo = post[1], post[3]
                    hw_t = self.dr["hwin_t_bf"]

                    def vz_step(part, t0=t0):
                        def loads():
                            slot, sbf = w8k.next()
                            P.dma("sp", slot, hw_t[j, 0][:, :, part * 512:(part + 1) * 512],
                                  reads=[self.castbuf_for("hwin_t", (j,), 0)], writes=[sbf])
                            return slot, sbf

                        def comp(h):
                            slot, sbf = h
                            for sbi in range(TT // 128):
                                bank, bb = banks.next()
                                for kc in range(8):
                                    P.op("pe", lambda e, kc=kc, sbi=sbi, bank=bank: e.matmul(
                                        bank, lhsT=hT[:, kc, sbi * 128:(sbi + 1) * 128], rhs=slot[:, kc, :],
                                        start=(kc == 0), stop=(kc == 7)), reads=[sbf, b_hT[kc]], writes=[bb])
                                r0 = t0 + sbi * 128
                                if part == 0:
                                    s, s_b = s16.next()
                                    P.op("act", lambda e, s=s, bank=bank: e.activation(out=s, in_=bank, func=AF.Copy), reads=[bb], writes=[s_b])
                                    P.dma("sp", self.dr["v"][NPo + r0:NPo + r0 + 128, :], s, reads=[s_b], writes=[self.DB("v")])
                                else:
                                    s, s_b = s32.next()
                                    P.op("dve", lambda e, s=s, bank=bank: e.tensor_copy(out=s, in_=bank), reads=[bb], writes=[s_b])
                                    P.dma("sp", self.dr["z"][r0:r0 + 128, :], s, reads=[s_b], writes=[self.DB("z")])
                        return (loads, comp)
                    steps.append(vz_step(0))
                    evs = []
                    for ci in range(16):
                        def ev(bank, bb, ci=ci, t0=t0):
                            if ci < 8:
                                s, sbf = s16.next()
                                name, r0, c0 = ("qT", ci * 128, t0) if ci < 4 else ("kT", (ci - 4) * 128, NPo + t0)
                            else:
                                s, sbf = s32.next()
                                name, r0, c0 = "xbcT", (ci - 8) * 128, NPo + t0
                            if ci % 2 == 0:
                                P.op("act", lambda e: e.activation(out=s, in_=bank, func=AF.Copy), reads=[bb], writes=[sbf])
                            else:
                                P.op("dve", lambda e: e.tensor_copy(out=s, in_=bank), reads=[bb], writes=[sbf])
                            P.dma("sp", self.dr[name][r0:r0 + 128, c0:c0 + TT], s, reads=[sbf], writes=[self.DB(name)])
                        evs.append((ci, ev))
                    for g0 in range(0, 16, 2):
                        steps.append(proj_fm_group("hwin_f", j, evs[g0:g0 + 2]))
                    steps.append(vz_step(1))

                    def comp_dt(h, t0=t0):
                        bank, bb = banks.next()
                        for sbi in range(TT // 128):
                            for kc in range(8):
                                P.op("pe", lambda e, kc=kc, sbi=sbi: e.matmul(
                                    bank[:, sbi * 8:(sbi + 1) * 8], lhsT=hT[:, kc, sbi * 128:(sbi + 1) * 128],
                                    rhs=wdt[:, kc, :], start=(kc == 0), stop=(kc == 7)),
                                    reads=[b_wdt, b_hT[kc]], writes=[bb])
                        P.op("dve", lambda e: e.tensor_copy(out=dts[:].rearrange("p a b -> p (a b)"), in_=bank[:, 0:32]),
                             reads=[bb], writes=[b_dts])
                        P.dma("sp", self.dr["dt"][NPo + t0:NPo + t0 + TT, :].rearrange("(s p) h -> p s h", p=128),
                              dts[:], reads=[b_dts], writes=[self.DB("dt")])
                    steps.append((None, comp_dt))
                ppos = min(len(steps), tile_first_step + 12)
                steps.insert(ppos, (None, lambda h, ti=ti: prefetch(ti + 1)))
                if pre is not None and pre[0] == "odd" and ti + 1 < len(tlist):
                    steps[ppos + 1:ppos + 1] = conv_steps(ti + 1)
                if post[0] != "final":
                    xd_ap = self.dr[x_dst].rearrange("(c p) t -> p c t", p=128)

                    def st_x(h, t0=t0, xslot=xslot, bxs=bxs):
                        P.dma("sp", xd_ap[:, :, t0:t0 + TT], xslot[:], reads=bxs, writes=[self.DB(x_dst)])
                    steps.append((None, st_x))
            if post[0] == "odd" and NP == 0:
                zs, zsb = s32.next()
                P.op("dve", lambda e: e.memset(zs[:, 0:16], 0.0), writes=[zsb])
                P.dma("sp", self.dr["vvT"].rearrange("(c p) t -> p c t", p=128)[:, :, 0:2],
                      zs[:, 0:16].rearrange("p (c t) -> p c t", c=8), reads=[zsb], writes=[self.DB("vvT")])
            run_steps(steps, depth=2)
            P.emit()


def host_weights(inp):
    f = lambda a: np.ascontiguousarray(np.asarray(a, dtype=np.float32))
    W = {}
    wg = np.stack([np.stack([f(inp["ffn1_wg"])[l], f(inp["ffn2_wg"])[l]]) for l in range(DEPTH)]).reshape(8, 1024, 2816)
    wu = np.stack([np.stack([f(inp["ffn1_wu"])[l], f(inp["ffn2_wu"])[l]]) for l in range(DEPTH)]).reshape(8, 1024, 2816)
    wdn = np.stack([np.stack([f(inp["ffn1_wd"])[l], f(inp["ffn2_wd"])[l]]) for l in range(DEPTH)]).reshape(8, 2816, 1024)

    def fm(w):
        lead = w.shape[:-2]
        K, F = w.shape[-2:]
        w = w.reshape(lead + (K // 128, 128, F // 128, 128))
        nd = len(lead)
        return np.ascontiguousarray(np.transpose(w, tuple(range(nd)) + (nd + 2, nd + 1, nd, nd + 3)))
    W["wgu"] = np.ascontiguousarray(np.stack([fm(wg), fm(wu)], axis=1))
    W["wd"] = fm(wdn)
    hw = f(inp["hyb_w_in"])
    cols_f = np.concatenate([np.arange(0, 1024), np.arange(2048, 3072)])
    cols_t = np.concatenate([np.arange(1024, 2048), np.arange(3072, 3080)])
    W["hwin_f"] = fm(hw[:, :, cols_f])
    wt = hw[:, :, cols_t].reshape(2, 8, 128, 1032)
    W["hwin_t"] = np.ascontiguousarray(np.transpose(wt, (0, 2, 1, 3)))[:, None]
    W["hwout"] = fm(f(inp["hyb_w_out"]))
    W["scwin"] = fm(f(inp["sc_w_in"]))
    W["scwout"] = fm(f(inp["sc_w_out"]))
    pv = np.zeros((128, PV_N), np.float32)
    norms = np.zeros((13, 1024), np.float32)
    for l in range(DEPTH):
        norms[3 * l + 0] = f(inp["ffn1_norm"])[l]
        norms[3 * l + 1] = f(inp["mix_norm"])[l]
        norms[3 * l + 2] = f(inp["ffn2_norm"])[l]
    norms[12] = f(inp["final_norm_w"])
    pv[:, PV_NORM:PV_NORM + 104] = norms.reshape(13, 8, 128).transpose(2, 0, 1).reshape(128, 104)
    cw = f(inp["ssm_conv_w"]).reshape(2, 4, 8, 128)
    pv[:, PV_CW:PV_CW + 64] = cw.transpose(3, 0, 2, 1).reshape(128, 64)
    cb = f(inp["ssm_conv_b"]).reshape(2, 8, 128)
    pv[:, PV_CB:PV_CB + 16] = cb.transpose(2, 0, 1).reshape(128, 16)
    scw = f(inp["sc_conv_w"]).reshape(2, 3, 8, 128)
    pv[:, PV_SCW:PV_SCW + 48] = scw.transpose(3, 0, 2, 1).reshape(128, 48)
    W["pvec"] = pv
    bv = np.zeros((1, BV_N), np.float32)
    for j in range(2):
        o = j * BV_LAYER
        bv[0, o:o + 64] = f(inp["diff_lq1"])[j]
        bv[0, o + 64:o + 128] = f(inp["diff_lk1"])[j]
        bv[0, o + 128:o + 192] = f(inp["diff_lq2"])[j]
        bv[0, o + 192:o + 256] = f(inp["diff_lk2"])[j]
        bv[0, o + 256:o + 384] = f(inp["diff_subln_w"])[j]
        bv[0, o + 384:o + 392] = f(inp["ssm_dt_bias"])[j]
        bv[0, o + 392:o + 400] = f(inp["ssm_a_log"])[j]
        bv[0, o + 400:o + 408] = f(inp["ssm_d"])[j]
        bv[0, o + 408:o + 920] = f(inp["ssm_norm_w"])[j]
    bv[0, BV_RB:BV_RB + 128] = f(inp["rel_bias"]).reshape(128)
    W["bvec"] = bv
    return W


WSHAPES = {"wgu": [8, 2, 22, 128, 8, 128], "wd": [8, 8, 128, 22, 128],
           "hwin_f": [2, 16, 128, 8, 128], "hwin_t": [2, 1, 128, 8, 1032],
           "hwout": [2, 8, 128, 8, 128], "scwin": [2, 24, 128, 8, 128],
           "scwout": [2, 8, 128, 8, 128]}


def declare_common(B, used):
    for n in used:
        B.weight(n, WSHAPES[n])
    B.ins.add("pvec")
    B.ins.add("bvec")
    B.D("pvec", [128, PV_N], F32)
    B.D("bvec", [1, BV_N], F32)


def _bc(ap, shape):
    return ap.to_broadcast(shape)


class Stage:
    def __init__(self, B, es):
        self.B, self.nc, self.P, self.es = B, B.nc, B.P, es

    def sb(self, name, shape, dt):
        return self.es.enter_context(self.nc.sbuf_tensor(U_(name), shape, dt))

    def banks(self, n=8):
        return Ring([self.es.enter_context(self.nc.psum_tensor(U_("bk"), [128, 512], F32))[:]
                     for i in range(n)])


def ssd_stage(B, j):
    nc, P, NT, NP, NK = B.nc, B.P, B.NT, B.NP, B.NK
    with ExitStack() as es:
        S = Stage(B, es)
        sb = S.sb
        banks = S.banks()
        xbt = sb("xbt", [128, 8, 515], F32); b_xbt = Buf("xbt")
        cacc = sb_ring(nc, es, "cacc", 2, [128, 512], F32)
        xsT = sb("xsT", [128, 4, 512], F32); b_xsT = [Buf("xsT") for _ in range(4)]
        bcT = sb("bcT", [128, 4, 512], BF16); b_bcT = [Buf("bcT") for _ in range(4)]
        xtok = sb("xtok", [128, 4, 512], F32); b_xtok = [Buf("xtok") for _ in range(4)]
        btok = sb("btok", [128, 4, 2, 128], BF16); b_btok = [Buf("btok") for _ in range(4)]
        dtr = sb("dtr", [128, 4, 8], F32); b_dtr = Buf("dtr")
        zt = sb("zt", [128, 4, 512], F32); b_zt = Buf("zt")
        bv = sb("bv", [128, BV_LAYER], F32); b_bv = Buf("bv")
        pv = sb("pv", [128, PV_N], F32); b_pv = Buf("pv")
        flag = sb("flag", [128, 1], F32); b_flag = Buf("flag")
        sm = sb("sm", [128, 24, 32], F32)
        b_sm = [Buf("sm%d" % i) for i in range(24)]
        U = sb("U", [128, 128], F32); ones = sb("ones", [128, 128], F32)
        idf = sb("idf", [128, 128], F32); idb = sb("idb", [128, 128], BF16)
        mneg = sb("mneg", [128, 4, 128], F32)
        b_const = Buf("const")
        dtile = sb("dtile", [128, 8, 64], F32); nwt = sb("nwt", [128, 512], F32)
        cst = sb("cst", [128, 4], F32)
        R_2 = [sb("R", [128, 8, 128], F32) for _ in range(2)]; b_R2 = [Buf("R") for _ in range(2)]
        Abm2 = [sb("Abm", [128, 8, 128], F32) for _ in range(2)]; b_Abm2 = [Buf("Abm") for _ in range(2)]
        eAB2 = [sb("eAB", [128, 8, 128], F32) for _ in range(2)]; b_eAB2 = [Buf("eAB") for _ in range(2)]
        L_2 = [sb("L", [128, 8, 128], F32) for _ in range(2)]; b_L2 = [Buf("L") for _ in range(2)]
        M_2 = [sb("M", [128, 8, 128], BF16) for _ in range(2)]; b_M2 = [Buf("M") for _ in range(2)]
        CsT2 = [sb("CsT", [128, 8, 128], BF16) for _ in range(2)]; b_CsT2 = [Buf("CsT") for _ in range(2)]
        X_2 = [sb("X", [128, 8, 64], BF16) for _ in range(2)]; b_X2 = [Buf("X") for _ in range(2)]
        Xd2 = [sb("Xd", [128, 8, 64], BF16) for _ in range(2)]; b_Xd2 = [Buf("Xd") for _ in range(2)]
        St = sb("St", [128, 8, 64], F32); b_St = Buf("St")
        Stmp = sb("Stmp", [128, 8, 64], F32); b_Stmp = Buf("Stmp")
        Sbf = sb("Sbf", [128, 8, 64], BF16); b_Sbf = Buf("Sbf")
        ysb2 = [sb("ysb", [128, 512], F32) for _ in range(2)]; b_ysb2 = [Buf("ysb") for _ in range(2)]
        ytmp2 = [sb("ytmp", [128, 512], F32) for _ in range(2)]; b_ytmp2 = [Buf("ytmp") for _ in range(2)]
        szt2 = [sb("szt", [128, 512], F32) for _ in range(2)]; b_szt2 = [Buf("szt") for _ in range(2)]
        junk2 = [sb("junk", [128, 256], F32) for _ in range(2)]; b_junk2 = [Buf("junk") for _ in range(2)]
        gn2 = [sb("gn", [128, 512], BF16) for _ in range(2)]; b_gn2 = [Buf("gn") for _ in range(2)]
        ymT = sb("ymT", [128, 4, 512], BF16); b_ymT = Buf("ymT")

        o = j * BV_LAYER
        P.dma("sp", bv[:], B.dr["bvec"][0:1, o:o + BV_LAYER].partition_broadcast(128), writes=[b_bv])
        P.dma("sp", pv[:], B.dr["pvec"], writes=[b_pv])
        P.dma("sp", flag[:], B.dr["flag"], writes=[b_flag])
        V = lambda f, **kw: P.op("dve", f, **kw)
        V(lambda e: e.memset(ones[:], 1.0), writes=[b_const])
        V(lambda e: e.memset(U[:], 1.0), writes=[b_const])
        V(lambda e: e.memset(idf[:], 1.0), writes=[b_const])
        V(lambda e: e.memset(mneg[:], 0.0), writes=[b_const])
        V(lambda e: e.memset(St[:], 0.0), writes=[b_St])
        V(lambda e: e.memset(Sbf[:], 0.0), writes=[b_Sbf])
        V(lambda e: e.memset(cst[:, 0:1], 1e-5), writes=[b_const])
        V(lambda e: e.memset(cst[:, 1:2], 1.0), writes=[b_const])
        P.op("pool", lambda e: e.affine_select(out=U[:], in_=U[:], pattern=[[1, 128]], compare_op=ALU.is_ge,
                                               fill=0.0, base=0, channel_multiplier=-1),
             reads=[b_const], writes=[b_const])
        P.op("pool", lambda e: e.affine_select(out=idf[:], in_=idf[:], pattern=[[1, 128]], compare_op=ALU.is_equal,
                                               fill=0.0, base=0, channel_multiplier=-1),
             reads=[b_const], writes=[b_const])
        for r in range(4):
            P.op("pool", lambda e, r=r: e.affine_select(out=mneg[:, r, :], in_=mneg[:, r, :], pattern=[[1, 128]],
                                                        compare_op=ALU.is_ge, fill=-1e30, base=0,
                                                        channel_multiplier=-1),
                 reads=[b_const], writes=[b_const])
        V(lambda e: e.tensor_copy(out=idb[:], in_=idf[:]), reads=[b_const], writes=[b_const])
        dtb = bv[:, 384:392]
        V(lambda e: e.tensor_copy(out=dtile[:], in_=_bc(bv[:, 400:408].unsqueeze(2), [128, 8, 64])),
          reads=[b_bv], writes=[b_const])
        V(lambda e: e.tensor_copy(out=nwt[:], in_=bv[:, 408:920]), reads=[b_bv], writes=[b_const])
        aneg = sm[:, 0, 0:8]
        P.op("act", lambda e: e.activation(out=aneg, in_=bv[:, 392:400], func=AF.Exp), reads=[b_bv], writes=[b_sm[0]])
        V(lambda e: e.tensor_scalar(out=aneg, in0=aneg, scalar1=-1.0, scalar2=None, op0=ALU.mult),
          reads=[b_sm[0]], writes=[b_sm[0]])

        xb_d = B.dr["xbcT"].rearrange("(c p) t -> p c t", p=128)
        nsb = NK // 512
        for sk in range(nsb):
            own = sk * 512 >= NP
            nch = 8 if own else 6
            c0 = sk * 512
            if sk == 0:
                V(lambda e: e.memset(xbt[:, :, 0:3], 0.0), writes=[b_xbt])
                P.dma("sp", xbt[:, 0:nch, 3:515], xb_d[:, 0:nch, 0:512], reads=[B.DB("xbcT")], writes=[b_xbt])
            else:
                P.dma("sp", xbt[:, 0:nch, :], xb_d[:, 0:nch, c0 - 3:c0 + 512], reads=[B.DB("xbcT")], writes=[b_xbt])
            P.dma("sp", dtr[:], B.dr["dt"][c0:c0 + 512, :].rearrange("(s p) h -> p s h", p=128),
                  reads=[B.DB("dt")], writes=[b_dtr])
            if own:
                P.dma("sp", zt[:], B.dr["z"][c0 - NP:c0 - NP + 512, :].rearrange("(s p) f -> p s f", p=128),
                      reads=[B.DB("z")], writes=[b_zt])
            for c in range(nch):
                a, ab = cacc.next()
                wc = PV_CW + (j * 8 + c) * 4
                bc_ = PV_CB + j * 8 + c
                V(lambda e, c=c, a=a, wc=wc, bc_=bc_: e.tensor_scalar(
                    out=a, in0=xbt[:, c, 3:515], scalar1=pv[:, wc + 3:wc + 4], scalar2=pv[:, bc_:bc_ + 1],
                    op0=ALU.mult, op1=ALU.add), reads=[b_xbt, b_pv], writes=[ab])
                for k in range(3):
                    V(lambda e, c=c, a=a, wc=wc, k=k: e.scalar_tensor_tensor(
                        out=a, in0=xbt[:, c, k:k + 512], scalar=pv[:, wc + k:wc + k + 1], in1=a,
                        op0=ALU.mult, op1=ALU.add), reads=[b_xbt, b_pv, ab], writes=[ab])
                if c < 4:
                    P.op("act", lambda e, c=c, a=a: e.activation(out=xsT[:, c, :], in_=a, func=AF.Silu),
                         reads=[ab], writes=[b_xsT[c]])
                else:
                    P.op("act", lambda e, c=c, a=a: e.activation(out=bcT[:, c - 4, :], in_=a, func=AF.Silu),
                         reads=[ab], writes=[b_bcT[c - 4]])
            xr, ax, ee, dtt, dA = [sm[:, i, :].rearrange("p (a b) -> p a b", a=4) for i in (1, 2, 3, 4, 5)]
            V(lambda e: e.tensor_tensor(out=xr, in0=dtr[:], in1=_bc(dtb.unsqueeze(1), [128, 4, 8]), op=ALU.add),
              reads=[b_dtr, b_bv], writes=[b_sm[1]])
            V(lambda e: e.scalar_tensor_tensor(out=ax, in0=xr, scalar=-1.0, in1=xr, op0=ALU.mult, op1=ALU.max), reads=[b_sm[1]], writes=[b_sm[2]])
            P.op("act", lambda e: e.activation(out=ee, in_=ax, func=AF.Exp, scale=-1.0), reads=[b_sm[2]], writes=[b_sm[3]])
            P.op("act", lambda e: e.activation(out=ee, in_=ee, func=AF.Ln, bias=cst[:, 1:2], scale=1.0),
                 reads=[b_sm[3], b_const], writes=[b_sm[3]])
            V(lambda e: e.scalar_tensor_tensor(out=dtt, in0=xr, scalar=0.0, in1=ee, op0=ALU.max, op1=ALU.add),
              reads=[b_sm[1], b_sm[3]], writes=[b_sm[4]])
            V(lambda e: e.tensor_tensor(out=dA, in0=dtt, in1=_bc(aneg.unsqueeze(1), [128, 4, 8]), op=ALU.mult),
              reads=[b_sm[4], b_sm[0]], writes=[b_sm[5]])
            for s4 in range(4):
                tb, tbb = banks.next()
                for c in range(4):
                    P.op("pe", lambda e, c=c, s4=s4, tb=tb: e.transpose(
                        out=tb[:, c * 128:(c + 1) * 128], in_=xsT[:, c, s4 * 128:(s4 + 1) * 128], identity=idf[:]),
                        reads=[b_xsT[c], b_const], writes=[tbb])
                P.op("act", lambda e, s4=s4, tb=tb: e.activation(out=xtok[:, s4, :], in_=tb, func=AF.Copy),
                     reads=[tbb], writes=[b_xtok[s4]])
                tb2, tbb2 = banks.next()
                tb2b = tb2.bitcast(BF16)
                for g in range(2):
                    P.op("pe", lambda e, g=g, s4=s4, tb2b=tb2b: e.transpose(
                        out=tb2b[:, g * 128:(g + 1) * 128], in_=bcT[:, g, s4 * 128:(s4 + 1) * 128], identity=idb[:]),
                        reads=[b_bcT[g], b_const], writes=[tbb2])
                V(lambda e, s4=s4, tb2b=tb2b: e.tensor_copy(out=btok[:, s4].rearrange("p g n -> p (g n)"), in_=tb2b[:, 0:256]),
                  reads=[tbb2], writes=[b_btok[s4]])
            def do_chunk(s4, sk=sk, own=own, c0=c0):
                par = s4 % 2
                R_, b_R = R_2[par], b_R2[par]
                Abm, b_Abm = Abm2[par], b_Abm2[par]
                eAB, b_eAB = eAB2[par], b_eAB2[par]
                L_, b_L = L_2[par], b_L2[par]
                M_, b_M = M_2[par], b_M2[par]
                CsT, b_CsT = CsT2[par], b_CsT2[par]
                X_, b_X = X_2[par], b_X2[par]
                Xd, b_Xd = Xd2[par], b_Xd2[par]
                ysb, b_ysb = ysb2[par], b_ysb2[par]
                ytmp, b_ytmp = ytmp2[par], b_ytmp2[par]
                szt, b_szt = szt2[par], b_szt2[par]
                junk, b_junk = junk2[par], b_junk2[par]
                gn, b_gn = gn2[par], b_gn2[par]
                cc = sk * 4 + s4
                pbank = [(banks.aps[4 * par + i], banks.bufs[4 * par + i]) for i in range(4)]
                cols = slice(s4 * 128, (s4 + 1) * 128)
                dAc = dA[:, s4, :]
                pa, pab = pbank[0][0][:, 256:512], pbank[0][1]
                P.op("pe", lambda e, pa=pa, dAc=dAc: e.matmul(pa[:, 0:8], lhsT=U[:], rhs=dAc, start=True, stop=True),
                     reads=[b_const, b_sm[5]], writes=[pab])
                V(lambda e, dAc=dAc: e.tensor_tensor(out=R_[:], in0=_bc(U[:].unsqueeze(1), [128, 8, 128]),
                                                      in1=_bc(dAc.unsqueeze(2), [128, 8, 128]), op=ALU.mult),
                  reads=[b_const, b_sm[5]], writes=[b_R])
                pb = [pbank[1], pbank[2]]
                for hh in range(2):
                    P.op("pe", lambda e, hh=hh, pb=pb: e.matmul(
                        pb[hh][0], lhsT=ones[:], rhs=R_[:, 4 * hh:4 * hh + 4, :].rearrange("p a b -> p (a b)"),
                        start=True, stop=True), reads=[b_const, b_R], writes=[pb[hh][1]])
                nA = sm[:, 6 + 6 * par, 0:8]
                V(lambda e, pa=pa, nA=nA: e.tensor_scalar(out=nA, in0=pa[:, 0:8], scalar1=-1.0, scalar2=None, op0=ALU.mult),
                  reads=[pab], writes=[b_sm[6 + 6 * par]])
                tot = sm[:, 7 + 6 * par, 0:8]
                for hh in range(2):
                    V(lambda e, hh=hh, pb=pb, tot=tot: e.tensor_copy(
                        out=tot[:, 4 * hh:4 * hh + 4],
                        in_=pb[hh][0].rearrange("p (a b) -> p a b", a=4)[:, :, 127]),
                      reads=[pb[hh][1]], writes=[b_sm[7 + 6 * par]])
                eT = sm[:, 8 + 6 * par, 0:8]
                P.op("act", lambda e, tot=tot, eT=eT: e.activation(out=eT, in_=tot, func=AF.Exp), reads=[b_sm[7 + 6 * par]], writes=[b_sm[8 + 6 * par]])
                dec = sm[:, 9 + 6 * par, 0:8]
                V(lambda e, tot=tot, nA=nA, dec=dec: e.tensor_tensor(out=dec, in0=tot, in1=nA, op=ALU.add),
                  reads=[b_sm[7 + 6 * par], b_sm[6 + 6 * par]], writes=[b_sm[9 + 6 * par]])
                P.op("act", lambda e, dec=dec: e.activation(out=dec, in_=dec, func=AF.Exp), reads=[b_sm[9 + 6 * par]], writes=[b_sm[9 + 6 * par]])
                scd = sm[:, 10 + 6 * par, 0:8]
                V(lambda e, dec=dec, scd=scd, s4=s4: e.tensor_tensor(out=scd, in0=dec, in1=dtt[:, s4, :], op=ALU.mult),
                  reads=[b_sm[9 + 6 * par], b_sm[4]], writes=[b_sm[10 + 6 * par]])
                xt3 = xtok[:, s4, :].rearrange("p (h d) -> p h d", h=8)
                V(lambda e, xt3=xt3, scd=scd: e.tensor_tensor(out=Xd[:], in0=xt3, in1=_bc(scd.unsqueeze(2), [128, 8, 64]), op=ALU.mult),
                  reads=[b_xtok[s4], b_sm[10 + 6 * par]], writes=[b_Xd])
                pst, pstb = pbank[3]
                for g in range(2):
                    P.op("pe", lambda e, g=g, pst=pst, s4=s4: e.matmul(
                        pst[:, g * 256:(g + 1) * 256], lhsT=btok[:, s4, g, :],
                        rhs=Xd[:, 4 * g:4 * g + 4, :].rearrange("p a b -> p (a b)"), start=True, stop=True),
                        reads=[b_btok[s4], b_Xd], writes=[pstb])
                if own:
                    for hh in range(2):
                        V(lambda e, hh=hh, pb=pb: e.tensor_tensor(
                            out=Abm[:, 4 * hh:4 * hh + 4, :].rearrange("p a b -> p (a b)"), in0=pb[hh][0],
                            in1=mneg[:].rearrange("p a b -> p (a b)"), op=ALU.add),
                          reads=[pb[hh][1], b_const], writes=[b_Abm])
                        P.op("act", lambda e, hh=hh, pb=pb: e.activation(
                            out=eAB[:, 4 * hh:4 * hh + 4, :].rearrange("p a b -> p (a b)"), in_=pb[hh][0], func=AF.Exp),
                            reads=[pb[hh][1]], writes=[b_eAB])
                    for h in range(8):
                        P.op("act", lambda e, h=h, nA=nA: e.activation(out=L_[:, h, :], in_=Abm[:, h, :], func=AF.Exp,
                                                                      bias=nA[:, h:h + 1], scale=1.0),
                             reads=[b_Abm, b_sm[6 + 6 * par]], writes=[b_L])
                    pcb, pcbb = pbank[0]
                    for g in range(2):
                        P.op("pe", lambda e, g=g, pcb=pcb, cols=cols: e.matmul(
                            pcb[:, g * 128:(g + 1) * 128], lhsT=bcT[:, g, cols], rhs=bcT[:, 2 + g, cols],
                            start=True, stop=True), reads=[b_bcT[g], b_bcT[2 + g]], writes=[pcbb])
                    for g in range(2):
                        V(lambda e, g=g, pcb=pcb: e.tensor_tensor(
                            out=M_[:, 4 * g:4 * g + 4, :], in0=_bc(pcb[:, g * 128:(g + 1) * 128].unsqueeze(1), [128, 4, 128]),
                            in1=L_[:, 4 * g:4 * g + 4, :], op=ALU.mult), reads=[pcbb, b_L], writes=[b_M])
                        V(lambda e, g=g, cols=cols: e.tensor_tensor(
                            out=CsT[:, 4 * g:4 * g + 4, :], in0=_bc(bcT[:, 2 + g, cols].unsqueeze(1), [128, 4, 128]),
                            in1=eAB[:, 4 * g:4 * g + 4, :], op=ALU.mult), reads=[b_bcT[2 + g], b_eAB], writes=[b_CsT])
                    V(lambda e, xt3=xt3, s4=s4: e.tensor_tensor(out=X_[:], in0=xt3, in1=_bc(dtt[:, s4, :].unsqueeze(2), [128, 8, 64]), op=ALU.mult),
                      reads=[b_xtok[s4], b_sm[4]], writes=[b_X])
                    P.mark()
                    py, pyb = pbank[1]
                    for h in range(8):
                        P.op("pe", lambda e, h=h, py=py: e.matmul(py[:, h * 64:(h + 1) * 64], lhsT=M_[:, h, :], rhs=X_[:, h, :],
                                                                 start=True, stop=False), reads=[b_M, b_X], writes=[pyb])
                        P.op("pe", lambda e, h=h, py=py: e.matmul(py[:, h * 64:(h + 1) * 64], lhsT=CsT[:, h, :], rhs=Sbf[:, h, :],
                                                                 start=False, stop=True), reads=[b_CsT, b_Sbf], writes=[pyb])
                    P.mark()
                    V(lambda e, s4=s4: e.tensor_tensor(out=ytmp[:], in0=xtok[:, s4, :], in1=dtile[:].rearrange("p a b -> p (a b)"), op=ALU.mult),
                      reads=[b_xtok[s4], b_const], writes=[b_ytmp])
                    V(lambda e, py=py: e.tensor_tensor(out=ysb[:], in0=py, in1=ytmp[:], op=ALU.add),
                      reads=[pyb, b_ytmp], writes=[b_ysb])
                    P.op("act", lambda e, s4=s4: e.activation(out=szt[:], in_=zt[:, s4, :], func=AF.Silu), reads=[b_zt], writes=[b_szt])
                    V(lambda e: e.tensor_tensor(out=ysb[:], in0=ysb[:], in1=szt[:], op=ALU.mult), reads=[b_ysb, b_szt], writes=[b_ysb])
                    ss = sm[:, 11 + 6 * par, 0:2]
                    for g in range(2):
                        P.op("act", lambda e, g=g, ss=ss: e.activation(out=junk[:], in_=ysb[:, g * 256:(g + 1) * 256], func=AF.Square,
                                                                      accum_out=ss[:, g:g + 1]), reads=[b_ysb], writes=[b_junk, b_sm[11 + 6 * par]])
                    P.op("act", lambda e, ss=ss: e.activation(out=ss, in_=ss, func=AF.Sqrt, bias=cst[:, 0:1], scale=1.0 / 256),
                         reads=[b_sm[11 + 6 * par], b_const], writes=[b_sm[11 + 6 * par]])
                    V(lambda e, ss=ss: e.reciprocal(out=ss, in_=ss), reads=[b_sm[11 + 6 * par]], writes=[b_sm[11 + 6 * par]])
                    for g in range(2):
                        V(lambda e, g=g, ss=ss: e.scalar_tensor_tensor(
                            out=gn[:, g * 256:(g + 1) * 256], in0=ysb[:, g * 256:(g + 1) * 256], scalar=ss[:, g:g + 1],
                            in1=nwt[:, g * 256:(g + 1) * 256], op0=ALU.mult, op1=ALU.mult),
                          reads=[b_ysb, b_sm[11 + 6 * par], b_const], writes=[b_gn])
                    pt, ptb = pbank[2]
                    ptb16 = pt.bitcast(BF16)
                    for c in range(4):
                        P.op("pe", lambda e, c=c, ptb16=ptb16: e.transpose(out=ptb16[:, c * 128:(c + 1) * 128], in_=gn[:, c * 128:(c + 1) * 128],
                                                                          identity=idb[:]), reads=[b_gn, b_const], writes=[ptb])
                    P.op("act", lambda e, ptb16=ptb16, cols=cols: e.activation(
                        out=ymT[:, :, cols], in_=ptb16[:, 0:512].rearrange("p (a b) -> p a b", a=4), func=AF.Copy),
                        reads=[ptb], writes=[b_ymT])
                P.mark()
                V(lambda e, eT=eT: e.tensor_tensor(out=Stmp[:], in0=St[:], in1=_bc(eT.unsqueeze(2), [128, 8, 64]), op=ALU.mult),
                  reads=[b_St, b_sm[8 + 6 * par]], writes=[b_Stmp])
                V(lambda e, pst=pst: e.tensor_tensor(out=St[:].rearrange("p a b -> p (a b)"), in0=pst, in1=Stmp[:].rearrange("p a b -> p (a b)"), op=ALU.add),
                  reads=[pstb, b_Stmp], writes=[b_St])
                if NP > 0 and cc == NP // 128 - 1:
                    V(lambda e: e.tensor_scalar(out=St[:], in0=St[:], scalar1=flag[:, 0:1], scalar2=None, op0=ALU.mult),
                      reads=[b_St, b_flag], writes=[b_St])
                P.op("act", lambda e: e.activation(out=Sbf[:], in_=St[:], func=AF.Copy), reads=[b_St], writes=[b_Sbf])
            for pr in range(2):
                caps = []
                for s4 in (2 * pr, 2 * pr + 1):
                    P.begin_capture()
                    do_chunk(s4)
                    parts = P.end_capture()
                    if len(parts) == 2:
                        parts = [parts[0], [], [], parts[1]]
                    assert len(parts) == 4, len(parts)
                    caps.append(parts)
                A_, B_ = caps

                def zipplay(x, y):
                    for i in range(max(len(x), len(y))):
                        if i < len(x):
                            P.replay(x[i])
                        if i < len(y):
                            P.replay(y[i])
                zipplay(A_[0], B_[0])
                for it in A_[1] + A_[3] + B_[1] + B_[3]:
                    P.replay(it)
                zipplay(A_[2], B_[2])
            if own:
                t0 = c0 - NP
                P.dma("sp", B.dr["mixT"][512:1024, t0:t0 + 512].rearrange("(c p) t -> p c t", p=128), ymT[:],
                      reads=[b_ymT], writes=[B.DB("mixT")])
        P.emit()


def attn_stage(B, j, layer):
    nc, P, NT, NP, NK = B.nc, B.P, B.NT, B.NP, B.NK
    lam_init = lam_init_of(layer)
    thr = t5_thresholds()
    NKB = NK // 128
    with ExitStack() as es:
        S = Stage(B, es)
        sb = S.sb
        sbanks = S.banks(4)
        abanks = Ring([es.enter_context(nc.psum_tensor(U_("ab"), [128, 512], F32))[:] for i in range(4)])
        KT = sb("KT", [128, NK], BF16); b_KT = Buf("KT")
        QT = sb("QT", [128, 2, NT], BF16); b_QT = Buf("QT")
        VA = sb("VA", [128, NKB, 136], BF16); b_VA = Buf("VA")
        bv = sb("bv", [128, BV_LAYER], F32); rb = sb("rb", [128, 32, 4], F32); b_bv = Buf("bv")
        flag = sb("flag", [128, 1], F32); b_flag = Buf("flag")
        dist = sb("dist", [128, 2, 128], F32); disti = sb("disti", [128, 2, 128], mybir.dt.int32)
        btile = sb("btile", [128, 4, 2, 256], F32); b_bt = Buf("bt")
        dl = sb("dl", [128, 32, 4], F32)
        tmpb = sb("tmpb", [128, 128], F32); b_tmpb = Buf("tmpb")
        sm = sb("sm", [128, 16, 8], F32); b_sm = [Buf("s") for _ in range(16)]
        subl = sb("subl", [128, 128], F32)
        idb = sb("idb", [128, 128], BF16); idf = sb("idf", [128, 128], F32)
        cst = sb("cst", [128, 2], F32)
        b_const = Buf("const")
        Er = sb_ring(nc, es, "Er", 3, [128, 512], BF16)
        Tr = sb_ring(nc, es, "Tr", 2, [128, 512], F32)
        cbt = sb("cbt", [128, 4, 3, 512], F32)
        o1 = sb("o1", [128, 128], F32); b_o1 = Buf("o1")
        oo = sb("oo", [128, 128], F32); b_oo = Buf("oo")
        junk = sb("junk", [128, 128], F32); b_junk = Buf("junk")
        on = sb("on", [128, 128], BF16); b_on = Buf("on")
        oT = sb("oT", [128, NT], BF16); b_oT = Buf("oT")
        V = lambda f, **kw: P.op("dve", f, **kw)
        A = lambda f, **kw: P.op("act", f, **kw)

        o = j * BV_LAYER
        P.dma("sp", bv[:], B.dr["bvec"][0:1, o:o + BV_LAYER].partition_broadcast(128), writes=[b_bv])
        P.dma("sp", rb[:].rearrange("p a b -> p (a b)"), B.dr["bvec"][0:1, BV_RB:BV_RB + 128].partition_broadcast(128), writes=[b_bv])
        P.dma("sp", flag[:], B.dr["flag"], writes=[b_flag])
        V(lambda e: e.memset(idf[:], 1.0), writes=[b_const])
        V(lambda e: e.memset(cst[:, 0:1], 1e-5), writes=[b_const])
        P.op("pool", lambda e: e.affine_select(out=idf[:], in_=idf[:], pattern=[[1, 128]], compare_op=ALU.is_equal,
                                               fill=0.0, base=0, channel_multiplier=-1), reads=[b_const], writes=[b_const])
        V(lambda e: e.tensor_copy(out=idb[:], in_=idf[:]), reads=[b_const], writes=[b_const])
        V(lambda e: e.memset(QT[:], 0.0), writes=[b_QT])
        V(lambda e: e.memset(VA[:, :, 128:129], 1.0), writes=[b_VA])
        if NP > 0:
            V(lambda e: e.tensor_copy(out=VA[:, 0:NP // 128, 128:129], in_=_bc(flag[:, 0:1].unsqueeze(1), [128, NP // 128, 1])),
              reads=[b_flag], writes=[b_VA])
        for i in range(2):
            V(lambda e, i=i: e.tensor_tensor(out=tmpb[:, 0:64], in0=bv[:, 128 * i:128 * i + 64], in1=bv[:, 128 * i + 64:128 * i + 128], op=ALU.mult),
              reads=[b_bv], writes=[b_tmpb])
            V(lambda e, i=i: e.tensor_reduce(out=sm[:, 0, i:i + 1], in_=tmpb[:, 0:64], axis=AX.X, op=ALU.add),
              reads=[b_tmpb], writes=[b_sm[0]])
        A(lambda e: e.activation(out=sm[:, 0, 0:2], in_=sm[:, 0, 0:2], func=AF.Exp), reads=[b_sm[0]], writes=[b_sm[0]])
        neglam = sm[:, 1, 0:1]
        V(lambda e: e.tensor_tensor(out=neglam, in0=sm[:, 0, 1:2], in1=sm[:, 0, 0:1], op=ALU.subtract), reads=[b_sm[0]], writes=[b_sm[1]])
        V(lambda e: e.tensor_scalar(out=neglam, in0=neglam, scalar1=-lam_init, scalar2=None, op0=ALU.add), reads=[b_sm[1]], writes=[b_sm[1]])
        V(lambda e: e.tensor_scalar(out=subl[:], in0=bv[:, 256:384], scalar1=1.0 - lam_init, scalar2=None, op0=ALU.mult),
          reads=[b_bv], writes=[b_const])
        for d in range(2):
            P.op("pool", lambda e, d=d: e.iota(disti[:, d, :], pattern=[[1, 128]], base=128 * d, channel_multiplier=-1),
                 writes=[b_const])
        V(lambda e: e.tensor_copy(out=dist[:], in_=disti[:]), reads=[b_const], writes=[b_const])
        V(lambda e: e.tensor_tensor(out=dl[:, 1:32, :], in0=rb[:, 1:32, :], in1=rb[:, 0:31, :], op=ALU.subtract), reads=[b_bv], writes=[b_const])
        for h in range(4):
            for d in range(2):
                bt = btile[:, h, d, 0:128]
                V(lambda e, bt=bt, h=h, d=d: e.tensor_scalar(out=bt, in0=dist[:, d, :], scalar1=0.0, scalar2=rb[:, 0, h:h + 1],
                                                            op0=ALU.mult, op1=ALU.add), reads=[b_const, b_bv], writes=[b_bt])
                for b in range(1, 32):
                    if d == 1 and thr[b] <= 1:
                        V(lambda e, bt=bt, h=h, b=b: e.tensor_scalar(out=bt, in0=bt, scalar1=dl[:, b, h:h + 1], scalar2=None, op0=ALU.add),
                          reads=[b_bt, b_const], writes=[b_bt])
                        continue
                    V(lambda e, h=h, d=d, b=b: e.tensor_scalar(out=tmpb[:], in0=dist[:, d, :], scalar1=float(thr[b]) - 0.5,
                                                               scalar2=dl[:, b, h:h + 1], op0=ALU.is_ge, op1=ALU.mult),
                      reads=[b_const], writes=[b_tmpb])
                    V(lambda e, bt=bt: e.tensor_tensor(out=bt, in0=bt, in1=tmpb[:], op=ALU.add), reads=[b_bt, b_tmpb], writes=[b_bt])
                if d == 0:
                    V(lambda e: e.tensor_scalar(out=tmpb[:], in0=dist[:, 0, :], scalar1=-0.5, scalar2=-30000.0,
                                                op0=ALU.is_lt, op1=ALU.mult), reads=[b_const], writes=[b_tmpb])
                    V(lambda e, bt=bt: e.tensor_tensor(out=bt, in0=bt, in1=tmpb[:], op=ALU.add), reads=[b_bt, b_tmpb], writes=[b_bt])
                V(lambda e, bt=bt, h=h, d=d: e.tensor_copy(out=btile[:, h, d, 128:256], in_=bt), reads=[b_bt], writes=[b_bt])

        for h in range(4):
            for half in range(2):
                o_ = half * 256
                V(lambda e, h=h, o_=o_: e.tensor_copy(out=cbt[:, h, 0, o_:o_ + 128], in_=btile[:, h, 1, 0:128]), reads=[b_bt], writes=[b_bt])
                V(lambda e, h=h, o_=o_: e.tensor_scalar(out=cbt[:, h, 0, o_ + 128:o_ + 256], in0=dist[:, 0, :], scalar1=0.0,
                                                       scalar2=rb[:, 31, h:h + 1], op0=ALU.mult, op1=ALU.add),
                  reads=[b_const, b_bv], writes=[b_bt])
                V(lambda e, h=h, o_=o_: e.tensor_copy(out=cbt[:, h, 1, o_:o_ + 128], in_=btile[:, h, 0, 0:128]), reads=[b_bt], writes=[b_bt])
                V(lambda e, h=h, o_=o_: e.tensor_copy(out=cbt[:, h, 1, o_ + 128:o_ + 256], in_=btile[:, h, 1, 0:128]), reads=[b_bt], writes=[b_bt])
                V(lambda e, h=h, o_=o_: e.memset(cbt[:, h, 2, o_:o_ + 128], -30000.0), writes=[b_bt])
                V(lambda e, h=h, o_=o_: e.tensor_copy(out=cbt[:, h, 2, o_ + 128:o_ + 256], in_=btile[:, h, 0, 0:128]), reads=[b_bt], writes=[b_bt])
        vd = B.dr["v"].rearrange("(kb p) f -> p kb f", p=128)
        qb0 = NP // 128
        import os
        DBG = int(os.environ.get("ATT_DEBUG", "9"))
        for h in range(4 if DBG >= 1 else 0):
            P.dma("sp", KT[:], B.dr["kT"][h * 128:(h + 1) * 128, :], reads=[B.DB("kT")], writes=[b_KT])
            P.dma("sp", QT[0:64, 0, :], B.dr["qT"][h * 128:h * 128 + 64, :], reads=[B.DB("qT")], writes=[b_QT])
            P.dma("sp", QT[64:128, 1, :], B.dr["qT"][h * 128 + 64:(h + 1) * 128, :], reads=[B.DB("qT")], writes=[b_QT])
            for k0 in range(0, NKB, 16):
                k1 = min(NKB, k0 + 16)
                P.dma("sp", VA[:, k0:k1, 0:128], vd[:, k0:k1, h * 128:(h + 1) * 128], reads=[B.DB("v")], writes=[b_VA])
            b31 = rb[:, 31, h:h + 1]
            LA = 2
            npair = NT // 256
            tasks = [(jp, kb) for jp in range(npair) for kb in range(qb0 + 2 * jp + 2)]
            acc, Es, deferred = {}, {}, []

            def flush(now, force=False):
                while deferred and (force or now - deferred[0][0] >= 2):
                    deferred.pop(0)[1]()

            def emit_S(i):
                jp, kb = tasks[i]
                qbA = qb0 + 2 * jp
                qc = slice(jp * 256, (jp + 1) * 256)
                kc = slice(kb * 128, (kb + 1) * 128)
                if kb == 0:
                    acc[jp] = [abanks.next() for _ in range(4)]
                sp_, spb = sbanks.next()
                P.op("pe", lambda e: e.matmul(sp_[:, 0:256], lhsT=KT[:, kc], rhs=QT[:, 0, qc], start=True, stop=True),
                     reads=[b_KT, b_QT], writes=[spb])
                P.op("pe", lambda e: e.matmul(sp_[:, 256:512], lhsT=KT[:, kc], rhs=QT[:, 1, qc], start=True, stop=True),
                     reads=[b_KT, b_QT], writes=[spb])
                E, Eb = Er.next()
                dd = qbA - kb
                hh_, b31_ = h, b31
                if dd >= 2:
                    A(lambda e: e.activation(out=E, in_=sp_, func=AF.Exp, bias=b31_, scale=0.125),
                      reads=[spb, b_bv], writes=[Eb])
                else:
                    case = 1 - dd
                    T, Tb = Tr.next()
                    V(lambda e: e.scalar_tensor_tensor(out=T, in0=sp_, scalar=0.125, in1=cbt[:, hh_, case, :],
                                                       op0=ALU.mult, op1=ALU.add), reads=[spb, b_bt], writes=[Tb])
                    A(lambda e: e.activation(out=E, in_=T, func=AF.Exp), reads=[Tb], writes=[Eb])
                Es[i] = (E, Eb)

            def emit_PV(i, now):
                jp, kb = tasks[i]
                qbA = qb0 + 2 * jp
                qbB = qbA + 1
                (a1, a1b), (a2, a2b), (c1, c1b), (c2, c2b) = acc[jp]
                E, Eb = Es.pop(i)
                if kb <= qbA:
                    P.op("pe", lambda e: e.matmul(a1[:, 0:129], lhsT=E[:, 0:128], rhs=VA[:, kb, 0:129],
                                                  start=(kb == 0), stop=(kb == qbA)), reads=[Eb, b_VA], writes=[a1b])
                    P.op("pe", lambda e: e.matmul(a2[:, 0:129], lhsT=E[:, 256:384], rhs=VA[:, kb, 0:129],
                                                  start=(kb == 0), stop=(kb == qbA)), reads=[Eb, b_VA], writes=[a2b])
                P.op("pe", lambda e: e.matmul(c1[:, 0:129], lhsT=E[:, 128:256], rhs=VA[:, kb, 0:129],
                                              start=(kb == 0), stop=(kb == qbB)), reads=[Eb, b_VA], writes=[c1b])
                P.op("pe", lambda e: e.matmul(c2[:, 0:129], lhsT=E[:, 384:512], rhs=VA[:, kb, 0:129],
                                              start=(kb == 0), stop=(kb == qbB)), reads=[Eb, b_VA], writes=[c2b])
                if kb == qbA:
                    epilogue(2 * jp, acc[jp][0], acc[jp][1], now)
                if kb == qbB:
                    epilogue(2 * jp + 1, acc[jp][2], acc[jp][3], now)
                    acc.pop(jp)

            def epilogue(jq, A1, A2, now):
                flush(now, force=True)
                a1, a1b = A1
                a2, a2b = A2
                qc = slice(jq * 128, (jq + 1) * 128)
                r = sm[:, 2 + (jq % 2) * 2, 0:4]
                rbuf = b_sm[2 + (jq % 2) * 2]
                V(lambda e: e.reciprocal(out=r[:, 0:1], in_=a1[:, 128:129]), reads=[a1b], writes=[rbuf])
                V(lambda e: e.reciprocal(out=r[:, 1:2], in_=a2[:, 128:129]), reads=[a2b], writes=[rbuf])
                V(lambda e: e.tensor_tensor(out=r[:, 1:2], in0=r[:, 1:2], in1=neglam, op=ALU.mult), reads=[rbuf, b_sm[1]], writes=[rbuf])
                V(lambda e: e.tensor_scalar(out=o1[:], in0=a1[:, 0:128], scalar1=r[:, 0:1], scalar2=None, op0=ALU.mult),
                  reads=[a1b, rbuf], writes=[b_o1])
                V(lambda e: e.scalar_tensor_tensor(out=oo[:], in0=a2[:, 0:128], scalar=r[:, 1:2], in1=o1[:], op0=ALU.mult, op1=ALU.add),
                  reads=[a2b, rbuf, b_o1], writes=[b_oo])
                A(lambda e: e.activation(out=junk[:], in_=oo[:], func=AF.Square, accum_out=r[:, 2:3]), reads=[b_oo], writes=[b_junk, rbuf])
                A(lambda e: e.activation(out=r[:, 2:3], in_=r[:, 2:3], func=AF.Sqrt, bias=cst[:, 0:1], scale=1.0 / 128),
                  reads=[rbuf, b_const], writes=[rbuf])
                V(lambda e: e.reciprocal(out=r[:, 2:3], in_=r[:, 2:3]), reads=[rbuf], writes=[rbuf])
                V(lambda e: e.scalar_tensor_tensor(out=on[:], in0=oo[:], scalar=r[:, 2:3], in1=subl[:], op0=ALU.mult, op1=ALU.mult),
                  reads=[b_oo, rbuf, b_const], writes=[b_on])

                def fin():
                    tp, tpb = sbanks.next()
                    tp16 = tp.bitcast(BF16)
                    P.op("pe", lambda e: e.transpose(out=tp16[:, 0:128], in_=on[:], identity=idb[:]), reads=[b_on, b_const], writes=[tpb])
                    A(lambda e: e.activation(out=oT[:, qc], in_=tp16[:, 0:128], func=AF.Copy), reads=[tpb], writes=[b_oT])
                deferred.append((now, fin))

            nt_ = len(tasks) if DBG >= 2 else 0
            for i in range(nt_ + LA if nt_ else 0):
                if i < nt_:
                    emit_S(i)
                if i >= LA:
                    emit_PV(i - LA, i)
                flush(i)
            flush(0, force=True)
            P.dma("sp", B.dr["mixT"][h * 128:(h + 1) * 128, :], oT[:], reads=[b_oT], writes=[B.DB("mixT")])
        P.emit()


def build_full(NT):
    B = Builder(NT, 0, ins=["xT_i", "flag"], outs=["outT"])
    declare_common(B, list(WSHAPES.keys()))
    B.D("xT_i", [1024, NT], F32)
    B.D("flag", [128, 1], F32)
    B.D("outT", [1024, NT], F32)
    B.D("xT", [1024, NT], F32)
    B.D("qT", [512, NT], BF16)
    B.D("kT", [512, NT], BF16)
    B.D("v", [NT, 512], BF16)
    B.D("z", [NT, 512], F32)
    B.D("xbcT", [1024, NT], F32)
    B.D("dt", [NT, 8], F32)
    B.D("mixT", [1024, NT], BF16)
    B.D("vvT", [1024, NT + 2], F32)
    B.D("bgT", [1024, NT], F32)
    def cast_ffn(fi):
        for g in range(2):
            B.cast("wgu", (fi, g), 2)
        B.cast("wd", (fi,), 2)
    cast_ffn(0)
    B.cast("hwin_f", (0,), 2)
    B.cast("hwin_t", (0,), 1)
    B.cast("hwout", (0,), 1)
    cast_ffn(1)
    cast_ffn(2)
    B.cast("scwin", (0,), 3)
    B.cast("scwout", (0,), 1)
    cast_ffn(3)
    cast_ffn(4)
    B.cast("hwin_f", (1,), 2)
    B.cast("hwin_t", (1,), 1)
    B.cast("hwout", (1,), 1)
    cast_ffn(5)
    cast_ffn(6)
    B.cast("scwin", (1,), 3)
    B.cast("scwout", (1,), 1)
    cast_ffn(7)
    import os
    nst = int(os.environ.get("NSTAGES", "99"))
    stages = [
        lambda: B.tl_segment("xT_i", "xT", None, [(0, 0)], ("even", 0, 1, 0)),
        lambda: ssd_stage(B, 0),
        lambda: attn_stage(B, 0, 0),
        lambda: B.tl_segment("xT", "xT", ("even", 0), [(1, 2), (2, 3)], ("odd", 0, 4)),
        lambda: B.tl_segment("xT", "xT", ("odd", 0), [(3, 5), (4, 6)], ("even", 1, 7, 0)),
        lambda: ssd_stage(B, 1),
        lambda: attn_stage(B, 1, 2),
        lambda: B.tl_segment("xT", "xT", ("even", 1), [(5, 8), (6, 9)], ("odd", 1, 10)),
        lambda: B.tl_segment("xT", None, ("odd", 1), [(7, 11)], ("final", 12)),
    ]
    for st in stages[:nst]:
        st()
    return B


def kernel(**inputs):
    x = np.asarray(inputs["x"], dtype=np.float32)
    Bsz, S, Dm = x.shape
    W = host_weights(inputs)
    Bd = build_full(S)
    flag = np.ones((128, 1), np.float32)
    xts = [np.ascontiguousarray(x[b].T) for b in range(Bsz)]
    owners = [0, 1, 4, 5]
    zW = {n: np.zeros_like(W[n]) for n in W}
    zx = np.zeros_like(xts[0])
    in_maps = []
    for c in range(8):
        if c in owners:
            m = {"xT_i": xts[owners.index(c)], "flag": flag}
            m.update(W)
        else:
            m = {"xT_i": zx, "flag": flag}
            m.update(zW)
        in_maps.append(m)
    res = run_bass_kernel_spmd(Bd.nc, in_maps, core_ids=list(range(8)))
    out = np.stack([np.ascontiguousarray(res.results[c]["outT"].T) for c in owners])
    return out.astype(np.float32)
```
